# Optimizing a Trainium2 kernel written in Bass

```python
import math
import jax, jax.numpy as jnp
from jax import lax
import numpy as np

D_MODEL = 1024
BATCH = 8
SEQ = 2048
DEPTH = 1
DEC_BATCH = 128
DEC_SEQ = 8
PAST_LEN = 16384
PAGE_SIZE = 128

D_MIX = D_MODEL
D_REC = D_MIX // 2
D_CONV = D_MIX - D_REC
H_REC = 4
DK = D_REC // H_REC
DV = D_REC // H_REC
CONV_W = 3
D_FF = 4 * D_MODEL
CHUNK = 64
N_MOD = 6
N_IN = 4 * D_REC + 3 * D_CONV
SPLITS = (D_REC, 2 * D_REC, 3 * D_REC, 4 * D_REC,
          4 * D_REC + D_CONV, 4 * D_REC + 2 * D_CONV)
EPS = 1e-6

kernel_name = "hymba_hgrn2_shortconv_adaln_step"


def rms_norm(x):
    xf = x.astype(jnp.float32)
    y = xf * lax.rsqrt(jnp.mean(xf * xf, axis=-1, keepdims=True) + EPS)
    return y.astype(x.dtype)


def hgrn2_chunked(q, k, logf, v, s0):
    b, l, h, dk = q.shape
    dv = v.shape[-1]
    c = math.gcd(l, CHUNK)
    n = l // c
    f32 = jnp.float32
    q = q.astype(f32).reshape(b, n, c, h, dk)
    k = k.astype(f32).reshape(b, n, c, h, dk)
    logf = logf.astype(f32).reshape(b, n, c, h, dk)
    v = v.astype(f32).reshape(b, n, c, h, dv)
    cum = jnp.cumsum(logf, axis=2)
    last = cum[:, :, -1:]
    q_dec = q * jnp.exp(cum)
    k_dec = k * jnp.exp(-cum)
    scores = jnp.einsum("bntHk,bnsHk->bnHts", q_dec, k_dec)
    causal = jnp.tril(jnp.ones((c, c), dtype=bool))
    scores = jnp.where(causal, scores, 0.0)
    o_intra = jnp.einsum("bnHts,bnsHv->bntHv", scores, v)
    k_end = k * jnp.exp(last - cum)
    upd = jnp.einsum("bnsHk,bnsHv->nbHkv", k_end, v)
    decay = jnp.exp(last[:, :, 0]).transpose(1, 0, 2, 3)

    def step(s, inp):
        d, u = inp
        return d[..., None] * s + u, s

    s_fin, s_prev = lax.scan(step, s0.astype(f32), (decay, upd))
    o_inter = jnp.einsum("bntHk,nbHkv->bntHv", q_dec, s_prev)
    o = (o_intra + o_inter).reshape(b, l, h, dv)
    return o, s_fin


def short_conv(u, buf, w):
    l = u.shape[1]
    full = jnp.concatenate([buf.astype(u.dtype), u], axis=1)
    y = w[0] * full[:, 0:l]
    for j in range(1, CONV_W):
        y = y + w[j] * full[:, j:j + l]
    return y, full[:, -(CONV_W - 1):]


def hybrid_layer(x, c, s_rec, s_conv, lb, w_ada, b_ada, w_in, w_conv, g_onorm,
                 w_out, w_up, w_down):
    bsz, l, _ = x.shape
    mod = jax.nn.silu(c) @ w_ada + b_ada
    sh1, sc1, g1, sh2, sc2, g2 = jnp.split(mod[:, None, :], N_MOD, axis=-1)

    h = rms_norm(x) * (1 + sc1) + sh1
    proj = h @ w_in
    q, fz, i, g, gb, gc, hv = jnp.split(proj, SPLITS, axis=-1)

    fz32 = fz.astype(jnp.float32).reshape(bsz, l, H_REC, DK)
    lb_h = lb.reshape(H_REC, DK)
    logf = jnp.log(lb_h + (1.0 - lb_h) * jax.nn.sigmoid(fz32))
    k = (1.0 - lb_h) * jax.nn.sigmoid(-fz32)
    o_rec, s_rec_new = hgrn2_chunked(q.reshape(bsz, l, H_REC, DK), k, logf,
                                     i.reshape(bsz, l, H_REC, DV), s_rec)
    o_rec = o_rec * lax.rsqrt(jnp.mean(o_rec * o_rec, axis=-1, keepdims=True) + EPS)
    o_rec = (o_rec.reshape(bsz, l, D_REC) * g_onorm).astype(x.dtype) * jax.nn.silu(g)

    y_conv, s_conv_new = short_conv(gc * hv, s_conv, w_conv)
    o_conv = gb * y_conv

    mix = jnp.concatenate([o_rec, o_conv], axis=-1) @ w_out
    x = x + g1 * mix

    h2 = rms_norm(x) * (1 + sc2) + sh2
    x = x + g2 * (jnp.square(jax.nn.relu(h2 @ w_up)) @ w_down)
    return x, s_rec_new.astype(s_rec.dtype), s_conv_new.astype(s_conv.dtype)


def setup_inputs(seed: int = 0) -> dict:
    key = jax.random.key(seed)
    ks = jax.random.split(key, 16)
    f32 = jnp.float32
    nrm = lambda k, shp, s: jax.random.normal(k, shp, f32) * s
    return {
        "x_prompt": nrm(ks[0], (BATCH, SEQ, D_MODEL), 1.0),
        "x_sample": nrm(ks[1], (DEC_BATCH, DEC_SEQ, D_MODEL), 1.0),
        "state_rec": nrm(ks[2], (DEPTH, DEC_BATCH, H_REC, DK, DV), 1.0),
        "state_conv": nrm(ks[3], (DEPTH, DEC_BATCH, CONV_W - 1, D_CONV), 1.0),
        "c_prompt": nrm(ks[4], (BATCH, D_MODEL), 1.0),
        "c_sample": nrm(ks[5], (DEC_BATCH, D_MODEL), 1.0),
        "lower_bounds": nrm(ks[6], (DEPTH + 1, D_REC), 0.1),
        "w_ada": nrm(ks[7], (DEPTH, D_MODEL, N_MOD * D_MODEL), 0.5 * D_MODEL ** -0.5),
        "b_ada": nrm(ks[8], (DEPTH, N_MOD * D_MODEL), 0.02),
        "w_in": nrm(ks[9], (DEPTH, D_MODEL, N_IN), D_MODEL ** -0.5),
        "w_conv": nrm(ks[10], (DEPTH, CONV_W, D_CONV), CONV_W ** -0.5),
        "g_onorm": 1.0 + nrm(ks[11], (DEPTH, D_REC), 0.02),
        "w_out": nrm(ks[12], (DEPTH, D_MIX, D_MODEL), D_MIX ** -0.5),
        "w_up": nrm(ks[13], (DEPTH, D_MODEL, D_FF), D_MODEL ** -0.5),
        "w_down": nrm(ks[14], (DEPTH, D_FF, D_MODEL), D_FF ** -0.5),
        "g_final": 1.0 + nrm(ks[15], (D_MODEL,), 0.02),
    }


def reference(x_prompt, x_sample, state_rec, state_conv, c_prompt, c_sample,
              lower_bounds, w_ada, b_ada, w_in, w_conv, g_onorm, w_out, w_up,
              w_down, g_final):
    lbs = jnp.cumsum(jax.nn.softmax(lower_bounds.astype(jnp.float32), axis=0), axis=0)
    xp, xs = x_prompt, x_sample
    rec_p, conv_p, rec_s, conv_s = [], [], [], []
    for layer in range(DEPTH):
        wl = (w_ada[layer], b_ada[layer], w_in[layer], w_conv[layer], g_onorm[layer],
              w_out[layer], w_up[layer], w_down[layer])
        s_rec0 = jnp.zeros((xp.shape[0], H_REC, DK, DV), state_rec.dtype)
        s_conv0 = jnp.zeros((xp.shape[0], CONV_W - 1, D_CONV), state_conv.dtype)
        xp, sr_p, sc_p = hybrid_layer(xp, c_prompt, s_rec0, s_conv0, lbs[layer], *wl)
        xs, sr_s, sc_s = hybrid_layer(xs, c_sample, state_rec[layer], state_conv[layer],
                                      lbs[layer], *wl)
        rec_p.append(sr_p)
        conv_p.append(sc_p)
        rec_s.append(sr_s)
        conv_s.append(sc_s)
    y_prompt = rms_norm(xp) * g_final
    y_sample = rms_norm(xs) * g_final
    new_rec_prompt = jnp.stack(rec_p)
    new_conv_prompt = jnp.stack(conv_p)
    new_rec_sample = jnp.stack(rec_s)
    new_conv_sample = jnp.stack(conv_s)
    return (y_prompt, y_sample, new_rec_prompt, new_conv_prompt, new_rec_sample, new_conv_sample)
```

```python
import math
from contextlib import ExitStack

import numpy as np
import concourse.bass as bass
import concourse.mybir as mybir
from concourse.ap import AP
from concourse.bass_utils import run_bass_kernel_spmd

F32 = mybir.dt.float32
BF16 = mybir.dt.bfloat16
ALU = mybir.AluOpType
AF = mybir.ActivationFunctionType
EPS = 1e-6
NCORES = 8
RING = 6

M64, M8, SC64, SC8, R64, R8, SELP, SELS, IDF, NCF = 0, 128, 256, 768, 896, 898, 914, 1042, 1170, 1298

ENGS = ["pe", "act", "dve", "pool", "sp"]


class Buf:
    __slots__ = ("name", "w", "r")

    def __init__(self, name):
        self.name = name
        self.w = None
        self.r = []


class Chan:
    __slots__ = ("sem", "count", "name")

    def __init__(self, name):
        self.name = name
        self.sem = None
        self.count = 0


class Op:
    __slots__ = ("eng", "fn", "waits", "signal", "chan", "cnt")


class Prog:
    def __init__(self, nc):
        self.nc = nc
        self.ops = {e: [] for e in ENGS}
        self.chans = []
        self.bufs = {}
        self.auto = {}

    def b(self, name):
        x = self.bufs.get(name)
        if x is None:
            x = self.bufs[name] = Buf(name)
        return x

    def chan(self, name):
        c = Chan(name)
        self.chans.append(c)
        return c

    def op(self, eng, fn, r=(), w=(), chan=None):
        reads = [self.b(x) if isinstance(x, str) else x for x in r]
        writes = [self.b(x) if isinstance(x, str) else x for x in w]
        if chan == "auto":
            key = eng + ((":in:" + writes[0].name) if writes else (":out:" + reads[0].name))
            chan = self.auto.get(key)
            if chan is None:
                chan = self.auto[key] = self.chan("a%d" % len(self.auto))
        o = Op()
        o.eng = eng
        o.fn = fn
        o.signal = False
        o.chan = chan
        deps = {}
        cdeps = {}

        def add(d, war=False):
            if d is None:
                return
            if d.chan is not None:
                cdeps[d.chan] = d.chan.count
                return
            if d.eng == eng and eng == "pe":
                return
            cur = deps.get(d.eng)
            if cur is None or d.cnt > cur.cnt:
                deps[d.eng] = d

        for bb in reads:
            add(bb.w)
        for bb in writes:
            add(bb.w)
            for rr in bb.r:
                add(rr, war=True)
        o.waits = []
        for d in deps.values():
            d.signal = True
            o.waits.append(("op", d))
        for c, n in cdeps.items():
            o.waits.append(("chan", c, n * 16))
        if chan is not None:
            chan.count += 1
        for bb in writes:
            bb.w = o
            bb.r = []
        for bb in reads:
            if bb not in writes:
                bb.r.append(o)
        o.cnt = len(self.ops[eng]) + 1
        self.ops[eng].append(o)
        return o

    def final_wait(self, eng="sp"):
        o = Op()
        o.eng = eng
        o.fn = lambda e: e.nop()
        o.signal = False
        o.chan = None
        o.cnt = 0
        o.waits = [("chan", c, c.count * 16) for c in self.chans if c.count > 0]
        self.ops[eng].append(o)

    def emit(self):
        nc = self.nc
        with ExitStack() as es:
            esem = {}
            for e in ["pe", "act", "dve", "pool"]:
                esem[e] = es.enter_context(nc.semaphore("sem_" + e))
            for c in self.chans:
                c.sem = es.enter_context(nc.semaphore("ch_" + c.name))
            for e in ENGS:
                k = 0
                for o in self.ops[e]:
                    if o.signal and o.chan is None:
                        k += 1
                        o.cnt = k
                    else:
                        o.cnt = -1
            block = es.enter_context(nc.Block())

            def run(engname, h):
                waited = {}
                for o in self.ops[engname]:
                    need = {}
                    for wt in o.waits:
                        if wt[0] == "chan":
                            sem, val = wt[1].sem, wt[2]
                        else:
                            d = wt[1]
                            sem, val = esem[d.eng], d.cnt
                        k = sem.num
                        if k not in need or need[k][1] < val:
                            need[k] = (sem, val)
                    for k, (sem, val) in need.items():
                        if waited.get(k, 0) >= val:
                            continue
                        h.wait_ge(sem, val)
                        waited[k] = val
                    ins = o.fn(h)
                    if o.chan is not None:
                        ins.then_inc(o.chan.sem, 16)
                    elif o.signal:
                        ins.then_inc(esem[engname], 1)

            @block.tensor
            def _(t):
                run("pe", t)

            @block.scalar
            def _(t):
                run("act", t)

            @block.vector
            def _(t):
                run("dve", t)

            @block.gpsimd
            def _(t):
                run("pool", t)

            @block.sync
            def _(t):
                run("sp", t)


def f_mm(specs):
    specs = list(specs)

    def fn(e):
        ins = None
        for (o_, l_, r_, st, sp_) in specs:
            ins = e.matmul(o_, lhsT=l_, rhs=r_, start=st, stop=sp_, skip_group_check=True)
        return ins
    return fn


def f_tr(specs):
    specs = list(specs)

    def fn(e):
        ins = None
        for (o_, i_, id_) in specs:
            ins = e.transpose(out=o_, in_=i_, identity=id_)
        return ins
    return fn


def f_act(out, in_, func, scale=None, bias=None, accum=None):
    kw = {}
    if scale is not None:
        kw["scale"] = scale
    if bias is not None:
        kw["bias"] = bias
    if accum is not None:
        kw["accum_out"] = accum
    return lambda e: e.activation(out=out, in_=in_, func=func, **kw)


def f_tt(out, in0, in1, op):
    return lambda e: e.tensor_tensor(out=out, in0=in0, in1=in1, op=op)


def f_ts(out, in0, s1, s2, op0, op1=None):
    if op1 is None:
        return lambda e: e.tensor_scalar(out=out, in0=in0, scalar1=s1, scalar2=None, op0=op0)
    return lambda e: e.tensor_scalar(out=out, in0=in0, scalar1=s1, scalar2=s2, op0=op0, op1=op1)


def f_stt(out, in0, scalar, in1, op0, op1):
    return lambda e: e.scalar_tensor_tensor(out=out, in0=in0, scalar=scalar, in1=in1, op0=op0, op1=op1)


def f_scan(out, d0, d1):
    return lambda e: e.tensor_tensor_scan(out=out, data0=d0, data1=d1, initial=0.0, op0=ALU.mult, op1=ALU.add)


def f_copy(out, in_):
    return lambda e: e.tensor_copy(out=out, in_=in_)


def f_acopy(out, in_):
    return lambda e: e.activation(out=out, in_=in_, func=AF.Copy)


def f_dma(out, in_):
    return lambda e: e.dma_start(out=out, in_=in_)


def f_memset(ap, val):
    return lambda e: e.memset(ap, val)


def build_program():
    nc = bass.Bass("TRN2", target_bir_lowering=False)

    def din(name, shape):
        return nc.dram_tensor(name, shape, F32, kind="ExternalInput").ap()

    def dout(name, shape):
        return nc.dram_tensor(name, shape, F32, kind="ExternalOutput").ap()

    xp_d = din("xp", [2048, 1024]); xs_d = din("xs", [128, 1024])
    srec_d = din("srec", [16, 4, 128, 128]); sconv_d = din("sconv", [16, 2, 512])
    cc_d = din("cc", [17, 1024]); lbnd_d = din("lbnd", [2, 512])
    wada_d = din("w_ada", [1024, 6144]); bada_d = din("b_ada", [6144])
    win_d = din("w_in", [1024, 3584]); wconv_d = din("w_conv", [3, 512]); gon_d = din("g_onorm", [512])
    wout_d = din("w_out", [1024, 1024]); wup_d = din("w_up", [1024, 4096]); wdn_d = din("w_down", [4096, 1024])
    gfin_d = din("g_final", [1024]); cF_d = din("constF", [128, NCF]); cB_d = din("constB", [128, 256])
    yp_d = dout("yp", [2048, 1024]); ys_d = dout("ys", [128, 1024]); recp_d = dout("recp", [4, 128, 128])
    convp_d = dout("convp", [2, 512]); recs_d = dout("recs", [16, 4, 128, 128]); convs_d = dout("convs", [16, 2, 512])

    P = Prog(nc)
    A = P.op
    es = ExitStack()
    es.enter_context(nc.allow_non_contiguous_dma(reason="small one-time strided vector loads / state rows"))

    def sb(name, shape, dt=F32):
        return es.enter_context(nc.sbuf_tensor(name, shape, dt))

    def ps(name, shape, dt=F32):
        return es.enter_context(nc.psum_tensor(name, shape, dt))

    cF = sb("cF", [128, NCF]); cB = sb("cB", [128, 256], BF16)
    ident = cB[:, 0:128]; onesb = cB[:, 128:256]
    vecT = sb("vecT", [128, 72]); lbv = sb("lbv", [128, 4]); oml = sb("oml", [128, 4]); noml = sb("noml", [128, 4])
    badaT = vecT[:, 0:48]; lbT = vecT[:, 48:56].rearrange("p (r h) -> p r h", h=4)
    wcT = vecT[:, 56:68].rearrange("p (j c) -> p j c", c=4); gon = vecT[:, 68:72]
    ucar = sb("ucar", [128, 4, 32]); uco = sb("uco", [128, 4, 32])
    gfin_b = sb("gfin_b", [128, 1024]); scT = sb("scT", [128, 8, 17], BF16); modT = sb("modT", [128, 48, 17])
    Gp = sb("Gp", [128, 2048]); Gs = sb("Gs", [128, 2048])
    arena = sb("arena", [128, 8192])
    Sseq = sb("Sseq", [128, 2, 512])
    xt = sb("xt", [128, 4, 1024]); nb = sb("nb", [128, 2, 1024], BF16)
    xs2 = sb("xs2", [128, 2, 1024])
    hA = sb("hA", [128, 8, 512], BF16); hB = sb("hB", [128, 8, 512], BF16)
    qdec = sb("qdec", [128, 4, 512], BF16); kdec = sb("kdec", [128, 4, 512], BF16); kendT = sb("kendT", [128, 4, 512], BF16)
    kend_tok = sb("kend_tok", [128, 4, 512], BF16); kend_m = sb("kend_m", [128, 2, 512], BF16)
    vtok = sb("vtok", [128, 4, 512], BF16); sm = sb("sm", [128, 2, 512], BF16)
    njunk = sm[:, :, :].rearrange("p a b -> p (a b)")
    Sst = sb("Sst", [128, 512]); Sb = sb("Sb", [128, 512], BF16); sq = sb("sq", [128, 2, 512], BF16)
    mixin = sb("mixin", [128, 8, 512], BF16)
    csb = sb("csb", [128, 512]); ubuf = sb("ubuf", [128, 514]); ucarry = sb("ucarry", [128, 4, 2]); ycv = sb("ycv", [128, 512])
    rl = sb("rl", [128, 2, 512], BF16); tmpo = sb("tmpo", [128, 1024])
    dec = sb("dec", [128, 4, 16]); st = sb("st", [128, 24])
    wr = sb("wr", [128, RING, 8, 512], BF16)

    def slab(i):
        return arena[:, i * 512:(i + 1) * 512]

    def slabB(i):
        return P.b(f"A{i}")

    uTf_all = arena[:].bitcast(BF16)

    def uTf(fc):
        return uTf_all[:, fc * 512:(fc + 1) * 512]

    qsb = lambda h: slab(8 + h)
    sg = lambda h: slab(12 + h)
    oT = lambda h: slab(h)

    PB = [ps(f"P{i}", [128, 512]) for i in range(3)]
    TR = ps("TR", [128, 1024], BF16)
    OB = [ps(f"O{i}", [128, 512]) for i in range(2)]
    SCb = ps("SC", [128, 512]); UPD = ps("UPD", [128, 512])
    bank_rr = [0]

    BANKS3 = [(PB[0], "P0"), (PB[1], "P1"), (PB[2], "P2")]
    BANKS6 = BANKS3 + [(OB[0], "O0"), (OB[1], "O1"), (UPD, "UPD")]
    bankset = [BANKS3]

    def nextbank():
        bs = bankset[0]
        i = bank_rr[0] % len(bs)
        bank_rr[0] += 1
        return bs[i]

    HA = [f"hA{i}" for i in range(4)]
    HB = [f"hB{i}" for i in range(4)]

    def warm(us, bank, bname):
        n = max(1, int(round(us / 0.107)))
        A("pe", f_mm([(bank[:, 0:256], cB[:, 0:128], cB[:, 0:256], True, True)] * n), r=["cB"], w=[bname])

    def rr(gens):
        gens = list(gens)
        while gens:
            for g in list(gens):
                try:
                    next(g)
                except StopIteration:
                    gens.remove(g)


    def wv(w_ap, c0):
        return w_ap.rearrange("(k p) n -> p k n", p=128)[:, :, c0:c0 + 512]

    wdn_v = wdn_d.rearrange("(a k p) n -> a p k n", k=8, p=128)
    blocks = []
    for j in range(4):
        blocks.append((("ada", j), wv(wada_d, j * 512)))
    for ti in range(5):
        for j in range(7):
            blocks.append((("in", ti, j), wv(win_d, j * 512)))
        if ti == 0:
            for j in range(4, 12):
                blocks.append((("ada", j), wv(wada_d, j * 512)))
        for j in range(2):
            blocks.append((("out", ti, j), wv(wout_d, j * 512)))
        for j in range(8):
            blocks.append((("up", ti, j), wv(wup_d, j * 512)))
        for dh in range(2):
            for a in range(4):
                blocks.append((("dn", ti, dh, a), wdn_v[a][:, :, dh * 512:(dh + 1) * 512]))
    ring = {"issue": 0, "use": 0}
    wscr = nc.dram_tensor("wscr", [25, 128, 4096], BF16, kind="Internal").ap()
    ch_ws = [P.chan(f"ws{i}") for i in range(RING)]
    tile_blk = {}
    for n_, (key_, _) in enumerate(blocks):
        if key_[0] != "ada" and key_[1] in (3, 4):
            tile_blk.setdefault(key_[1], []).append(n_)

    def ring_issue():
        n = ring["issue"]
        if n >= len(blocks):
            return
        s = n % RING
        key = blocks[n][0]
        if key[0] != "ada" and key[1] == 4:
            b_ = tile_blk[4].index(n)
            A("pool", f_dma(wr[:, s].rearrange("p k n -> p (k n)"), wscr[b_]), r=[f"wscr{b_}"], w=[f"wr{s}"], chan="auto")
        else:
            A("pool", f_dma(wr[:, s], blocks[n][1]), w=[f"wr{s}"], chan="auto")
        ring["issue"] = n + 1

    def ring_take(key):
        n = ring["use"]
        assert blocks[n][0] == key, (blocks[n][0], key)
        ring["use"] = n + 1
        s = n % RING
        if key[0] != "ada" and key[1] == 3:
            b_ = tile_blk[3].index(n)
            A("sp", f_dma(wscr[b_], wr[:, s].rearrange("p k n -> p (k n)")), r=[f"wr{s}"], w=[f"wscr{b_}"], chan=ch_ws[s])
        return s

    A("sp", f_dma(cF[:], cF_d[:, :]), w=["cF"], chan="auto")
    A("pool", f_dma(cB[:], cB_d[:, :]), w=["cB"], chan="auto")
    for _ in range(RING):
        ring_issue()
    cin = arena[0:17, 0:1024]; ctmp = arena[0:17, 1024:2048]; gtok = arena[0:17, 2048:4096]
    A("sp", f_dma(cin, cc_d[:, :]), w=["A0", "A1"], chan="auto")
    A("dve", f_memset(csb[:, 0:256], 0.0), w=["csb"])
    A("dve", f_memset(ycv[:, :], 0.0), w=["ycv"])
    A("sp", f_dma(csb[0:48, 0:128], bada_d.rearrange("(c p) -> c p", p=128)), w=["csb"], chan="auto")
    A("sp", f_dma(csb[48:56, 0:128], lbnd_d.rearrange("r (h p) -> (r h) p", p=128)), w=["csb"], chan="auto")
    A("sp", f_dma(csb[56:64, 0:128], wconv_d.rearrange("j (c p) -> (j c) p", p=128)[0:8]), w=["csb"], chan="auto")
    A("sp", f_dma(csb[0:4, 128:256], wconv_d.rearrange("j (c p) -> (j c) p", p=128)[8:12]), w=["csb"], chan="auto")
    A("sp", f_dma(csb[4:8, 128:256], gon_d.rearrange("(h p) -> h p", p=128)), w=["csb"], chan="auto")
    A("sp", f_dma(ycv[0:32, :], sconv_d.rearrange("s j c -> (s j) c")), w=["ycv"], chan="auto")
    vb, vbn = nextbank()
    A("pe", f_tr([(vb[:, 0:128], csb[:, 0:128], cF[:, IDF:IDF + 128]),
                  (vb[:, 128:256], csb[:, 128:256], cF[:, IDF:IDF + 128])]), r=["csb", "cF"], w=[vbn])
    A("dve", f_copy(vecT[:, 0:64], vb[:, 0:64]), r=[vbn], w=["badaT", "lbT", "wcT", "gon"])
    A("dve", f_copy(vecT[:, 64:72], vb[:, 128:136]), r=[vbn], w=["badaT", "lbT", "wcT", "gon"])
    vb, vbn = nextbank()
    A("pe", f_tr([(vb[:, cc * 128:(cc + 1) * 128], ycv[:, cc * 128:(cc + 1) * 128], cF[:, IDF:IDF + 128]) for cc in range(4)]),
      r=["ycv", "cF"], w=[vbn])
    A("dve", f_copy(ucar[:], vb[:, :].rearrange("p (c t) -> p c t", t=128)[:, :, 0:32]), r=[vbn], w=["ucar"])
    A("sp", f_dma(gfin_b[:], gfin_d.partition_broadcast(128)), w=["gfin_b"], chan="auto")

    A("act", f_act(ctmp, cin, AF.Exp, scale=-1.0), r=["A0", "A1"], w=["A2", "A3"])
    A("act", f_act(ctmp, ctmp, AF.Ln, bias=1.0), r=["A2", "A3"], w=["A2", "A3"])
    A("act", f_act(ctmp, ctmp, AF.Exp, scale=-1.0), r=["A2", "A3"], w=["A2", "A3"])
    A("dve", f_tt(nb[0:17, 0, :], cin, ctmp, ALU.mult), r=["A0", "A1", "A2", "A3"], w=["nb0"])
    A("pe", f_tr([(TR[:, j * 32:j * 32 + 17], nb[0:17, 0, j * 128:(j + 1) * 128], cB[0:17, 0:17]) for j in range(8)]),
      r=["nb0", "cB"], w=["TR"])
    A("act", f_acopy(scT[:], TR[:, 0:256].rearrange("p (a b) -> p a b", b=32)[:, :, 0:17]), r=["TR"], w=["scT"])
    A("dve", f_tt(lbv[:], lbT[:, 1, :], lbT[:, 0, :], ALU.subtract), r=["lbT"], w=["lbv"])
    A("act", f_act(lbv[:], lbv[:], AF.Exp), r=["lbv"], w=["lbv"])
    A("act", f_act(lbv[:], lbv[:], AF.Ln, bias=1.0), r=["lbv"], w=["lbv"])
    A("act", f_act(lbv[:], lbv[:], AF.Exp, scale=-1.0), r=["lbv"], w=["lbv"])
    A("dve", f_ts(oml[:], lbv[:], -1.0, 1.0, ALU.mult, ALU.add), r=["lbv"], w=["oml"])
    A("dve", f_ts(noml[:], oml[:], -1.0, None, ALU.mult), r=["oml"], w=["noml"])
    A("dve", f_ts(gon[:], gon[:], math.sqrt(128.0), None, ALU.mult), r=["gon"], w=["gon"])
    A("dve", f_memset(ucarry[:], 0.0), w=["ucarry"])
    A("dve", f_memset(Sst[:], 0.0), w=["Sst"])
    A("dve", f_memset(Sseq[:, 0, :], 0.0), w=["Sseq0"])
    A("dve", f_memset(Sb[:], 0.0), w=["Sb"])

    FM_GROUPS = {0: (0, 0), 1: (0, 4), 2: (0, 8), 3: (0, 12), 6: (1, 0), 7: (1, 4), 8: (1, 8), 9: (1, 12)}
    fm_bank = {}
    gtok2 = xs2[0:17, :, :].rearrange("p a b -> p (a b)")

    def mod_blocks(bis):
        for bi in bis:
            s = ring_take(("ada", bi))
            if bi in FM_GROUPS:
                grp, c0 = FM_GROUPS[bi]
                if c0 == 0:
                    fm_bank[grp] = nextbank()
                bank, bname = fm_bank[grp]
                for cj in range(4):
                    col = (c0 + cj) * 17
                    A("pe", f_mm([(bank[:, col:col + 17], wr[:, s, k, cj * 128:(cj + 1) * 128], scT[:, k, :], k == 0, k == 7)
                                  for k in range(8)]), r=[f"wr{s}", "scT"], w=[bname])
                if c0 == 12:
                    j0 = 0 if grp == 0 else 24
                    A("dve", f_tt(modT[:, j0:j0 + 16, :], bank[:, 0:272].rearrange("p (a b) -> p a b", b=17),
                                  badaT[:, j0:j0 + 16].unsqueeze(2).to_broadcast([128, 16, 17]), ALU.add),
                      r=[bname, "badaT"], w=["modT"])
                    A("dve", f_ts(modT[:, j0 + 8:j0 + 16, :], modT[:, j0 + 8:j0 + 16, :], 1.0, None, ALU.add), r=["modT"], w=["modT"])
            else:
                half = {4: 0, 5: 1, 10: 2, 11: 3}[bi]
                bank, bname = nextbank()
                A("pe", f_mm([(bank[0:17, :], scT[:, k, :], wr[:, s, k, :], k == 0, k == 7) for k in range(8)]),
                  r=[f"wr{s}", "scT"], w=[bname])
                sl = [f"xs2_{half // 2}"]
                A("dve", f_tt(gtok2[:, half * 512:(half + 1) * 512], bank[0:17, :], gtok2[:, half * 512:(half + 1) * 512], ALU.add),
                  r=[bname] + sl, w=sl)
            ring_issue()

    def setup_part2():
        A("sp", f_dma(gtok2[:, 0:1024], bada_d[2048:3072].partition_broadcast(17)), w=["xs2_0"], chan="auto")
        A("sp", f_dma(gtok2[:, 1024:2048], bada_d[5120:6144].partition_broadcast(17)), w=["xs2_1"], chan="auto")
        mod_blocks(range(4, 12))
        for q in range(4):
            for (Gt, gname, sel0) in ((Gp, "Gp", SELP), (Gs, "Gs", SELS)):
                bank, bname = nextbank()
                A("pe", f_mm([(bank[:, :], cF[0:17, sel0:sel0 + 128], gtok2[:, q * 512:(q + 1) * 512], True, True)]),
                  r=["cF", f"xs2_{q // 2}"], w=[bname])
                A("act", f_acopy(Gt[:, q * 512:(q + 1) * 512], bank[:, :]), r=[bname], w=[gname])

    mod_blocks(range(0, 4))

    TRB = [(TR, "TR"), (SCb[:, :].bitcast(BF16), "SC")]

    def rstd_chain(src_ap, srcbufs, col):
        sn = f"st{col}"
        A("act", f_act(njunk, src_ap, AF.Square, accum=st[:, col:col + 1]), r=srcbufs + [sn], w=["sm0", "sm1", sn])
        A("act", f_act(st[:, col:col + 1], st[:, col:col + 1], AF.Ln, scale=1.0 / 1024, bias=EPS), r=[sn], w=[sn])
        A("act", f_act(st[:, col:col + 1], st[:, col:col + 1], AF.Exp, scale=-0.5), r=[sn], w=[sn])

    def norm_stage1(src_ap, srcbufs, s, col):
        rstd_chain(src_ap, srcbufs, col)
        A("act", f_act(nb[:, s % 2, :], src_ap, AF.Copy, scale=st[:, col:col + 1]), r=srcbufs + [f"st{col}"], w=[f"nb{s % 2}"])

    def norm_stage2(kind, s, hdst, hname, j_sh, j_sc, warm_us=0.0):
        trt, trn = TRB[s % 2]
        if warm_us > 0:
            warm(warm_us, TR[:, :].bitcast(F32) if trn == "TR" else SCb, trn)
        if trn == "TR":
            trap = lambda a, b: TR[:, a:b]
        else:
            trap = lambda a, b: trt[:, a:b]
        A("pe", f_tr([(trap(j * 128, (j + 1) * 128), nb[:, s % 2, j * 128:(j + 1) * 128], ident) for j in range(8)]),
          r=[f"nb{s % 2}", "cB"], w=[trn])
        if kind == "p":
            for j in range(8):
                A("dve", f_stt(hdst[:, j, s * 128:(s + 1) * 128], trap(j * 128, (j + 1) * 128), modT[:, j_sc + j, 0:1],
                               modT[:, j_sh + j, 0:1].to_broadcast([128, 128]), ALU.mult, ALU.add), r=[trn, "modT"], w=[f"{hname}{s}"])
        else:
            trv = trap(0, 1024).rearrange("p (j s t) -> p j s t", s=16, t=8)
            tmv = tmpo[:, :].rearrange("p (j s t) -> p j s t", s=16, t=8)
            hv = hdst[:, :, 0:128].rearrange("p j (s t) -> p j s t", t=8)
            scb = AP(modT, j_sc * 17 + 1, [[816, 128], [17, 8], [1, 16], [0, 8]])
            shb = AP(modT, j_sh * 17 + 1, [[816, 128], [17, 8], [1, 16], [0, 8]])
            A("dve", f_tt(tmv, trv, scb, ALU.mult), r=[trn, "modT", "tmpo0", "tmpo1"], w=["tmpo0", "tmpo1"])
            A("dve", f_tt(hv, tmv, shb, ALU.add), r=["tmpo0", "tmpo1", "modT"], w=[f"{hname}{s}"])

    def tile_x(kind, ti):
        return xp_d[ti * 512:(ti + 1) * 512, :] if kind == "p" else xs_d

    def n1_stage1(kind, ti, s):
        xd = tile_x(kind, ti)
        A("sp", f_dma(xs2[:, s % 2, :], xd[s * 128:(s + 1) * 128, :]), w=[f"xs2_{s % 2}"], chan="auto")
        norm_stage1(xs2[:, s % 2, :], [f"xs2_{s % 2}"], s, s)

    def n1_stage2(kind, s):
        norm_stage2(kind, s, hA, "hA", 0, 8)

    def process_tile(kind, ti, tidx, nxt):
        TT = 512 if kind == "p" else 128
        NS = TT // 128
        C = 64 if kind == "p" else 8
        NCH = TT // C
        NCB = 128 // C
        xd = xp_d[ti * 512:(ti + 1) * 512, :] if kind == "p" else xs_d
        yd = yp_d[ti * 512:(ti + 1) * 512, :] if kind == "p" else ys_d
        G = Gp if kind == "p" else Gs
        gname = "Gp" if kind == "p" else "Gs"
        mcol = M64 if kind == "p" else M8
        sccol = SC64 if kind == "p" else SC8
        rcol = R64 if kind == "p" else R8
        last_prompt = (kind == "p" and ti == 3)

        for s in range(NS):
            A("sp", f_dma(xt[:, s, :], xd[s * 128:(s + 1) * 128, :]), w=[f"xt{s}"], chan="auto")
        hT = hA
        bankset[0] = BANKS6
        bank_rr[0] = 0
        A("dve", f_memset(st[:, 4:12], 0.0), w=[f"st{c_}" for c_ in range(4, 12)])

        s_ = ring_take(("in", tidx, 0))
        for h in range(4):
            bank, bname = nextbank()
            A("pe", f_mm([(bank[:, 0:TT], wr[:, s_, k, h * 128:(h + 1) * 128], hT[:, k, 0:TT], k == 0, k == 7) for k in range(8)]),
              r=[f"wr{s_}"] + HA, w=[bname])
            A("act", f_acopy(qsb(h)[:, 0:TT], bank[:, 0:TT]), r=[bname], w=[f"A{8 + h}"])
        ring_issue()
        s_ = ring_take(("in", tidx, 1))
        if kind == "p":
            warm(1.5, TR[:, :].bitcast(F32), "TR")

        def fchain(h, bank, bname):
            b0 = 4 * (h % 2)
            S0, S1, S2, S3 = (slab(b0 + i)[:, 0:TT] for i in range(4))
            n0, n1, n2, n3 = (f"A{b0 + i}" for i in range(4))
            A("act", f_act(S0, bank[:, 0:TT], AF.Exp, scale=-1.0), r=[bname], w=[n0]); yield
            A("act", f_act(S1, S0, AF.Ln, bias=1.0), r=[n0], w=[n1]); yield
            A("act", f_act(S1, S1, AF.Exp, scale=-1.0), r=[n1], w=[n1]); yield
            A("act", f_act(S2, S1, AF.Ln, scale=oml[:, h:h + 1], bias=lbv[:, h:h + 1]), r=[n1, "oml", "lbv"], w=[n2])
            A("dve", f_ts(S3, S1, noml[:, h:h + 1], oml[:, h:h + 1], ALU.mult, ALU.add), r=[n1, "noml", "oml"], w=[n3]); yield
            A("dve", f_scan(S0, cF[:, sccol:sccol + TT], S2), r=["cF", n2], w=[n0]); yield
            A("act", f_act(S1, S0, AF.Exp), r=[n0], w=[n1])
            A("act", f_act(S2, S0, AF.Exp, scale=-1.0), r=[n0], w=[n2]); yield
            A("dve", f_tt(qdec[:, h, 0:TT], qsb(h)[:, 0:TT], S1, ALU.mult), r=[f"A{8 + h}", n1], w=[f"qdec{h}"])
            A("dve", f_tt(kdec[:, h, 0:TT], S3, S2, ALU.mult), r=[n3, n2], w=[f"kdec{h}"])
            cum3 = S0.rearrange("p (c t) -> p c t", t=C)
            A("act", f_act(dec[:, h, 0:NCH], cum3[:, :, C - 1], AF.Exp), r=[n0], w=["dec"]); yield
            A("dve", f_tt(S1.rearrange("p (c t) -> p c t", t=C), S2.rearrange("p (c t) -> p c t", t=C),
                          dec[:, h, 0:NCH].unsqueeze(2).to_broadcast([128, NCH, C]), ALU.mult), r=[n2, "dec", n1], w=[n1]); yield
            A("dve", f_tt(kendT[:, h, 0:TT], S3, S1, ALU.mult), r=[n3, n1], w=[f"kendT{h}"]); yield

        for hp in range(2):
            gens = []
            for h in (2 * hp, 2 * hp + 1):
                bank, bname = nextbank()
                A("pe", f_mm([(bank[:, 0:TT], wr[:, s_, k, h * 128:(h + 1) * 128], hT[:, k, 0:TT], k == 0, k == 7) for k in range(8)]),
                  r=[f"wr{s_}"] + HA, w=[bname])
                gens.append(fchain(h, bank, bname))
            rr(gens)
        ring_issue()
        if kind == "p":
            warm(2.0, TR[:, :].bitcast(F32), "TR")
        s_ = ring_take(("in", tidx, 2))
        for s in range(NS):
            bank, bname = nextbank()
            A("pe", f_mm([(bank[:, :], hT[:, k, s * 128:(s + 1) * 128], wr[:, s_, k, :], k == 0, k == 7) for k in range(8)]),
              r=[f"wr{s_}"] + HA, w=[bname])
            A("act", f_acopy(vtok[:, s, :], bank[:, :]), r=[bname], w=[f"vtok{s}"])
        ring_issue()
        s_ = ring_take(("in", tidx, 3))

        def gchain(h, bank, bname):
            si = [0, 1, 4, 5][h]
            S0 = slab(si)[:, 0:TT]
            n0 = f"A{si}"
            A("act", f_act(S0, bank[:, 0:TT], AF.Exp, scale=-1.0), r=[bname], w=[n0]); yield
            A("act", f_act(S0, S0, AF.Ln, bias=1.0), r=[n0], w=[n0]); yield
            A("act", f_act(S0, S0, AF.Exp, scale=-1.0), r=[n0], w=[n0]); yield
            A("dve", f_tt(sg(h)[:, 0:TT], bank[:, 0:TT], S0, ALU.mult), r=[bname, n0], w=[f"A{12 + h}"]); yield

        gens = []
        for h in range(4):
            bank, bname = nextbank()
            A("pe", f_mm([(bank[:, 0:TT], wr[:, s_, k, h * 128:(h + 1) * 128], hT[:, k, 0:TT], k == 0, k == 7) for k in range(8)]),
              r=[f"wr{s_}"] + HA, w=[bname])
            gens.append(gchain(h, bank, bname))
        rr(gens)
        ring_issue()
        bankset[0] = BANKS3
        bank_rr[0] = 0
        sB = ring_take(("in", tidx, 4)); sC = ring_take(("in", tidx, 5)); sH = ring_take(("in", tidx, 6))

        def conv_cc(cc):
            bB, nB = nextbank(); bC, nC = nextbank(); bH, nH = nextbank()
            for (bk, nm, sl) in ((bB, nB, sB), (bC, nC, sC), (bH, nH, sH)):
                A("pe", f_mm([(bk[:, 0:TT], wr[:, sl, k, cc * 128:(cc + 1) * 128], hT[:, k, 0:TT], k == 0, k == 7) for k in range(8)]),
                  r=[f"wr{sl}"] + HA, w=[nm])
                yield
            if kind == "p":
                uv = ubuf[:, :].unsqueeze(1)
                L = 512
                v3 = lambda t: t.unsqueeze(1)
                A("pool", f_copy(ubuf[:, 0:2], ucarry[:, cc, :]), r=["ucarry"], w=["ubuf"])
            else:
                uv = ubuf[:, 0:160].rearrange("p (s t) -> p s t", t=10)
                L = 8
                v3 = lambda t: t.rearrange("p (s t) -> p s t", t=8)
                A("pool", f_copy(uv[:, :, 0:2], ucar[:, cc, :].rearrange("p (s j) -> p s j", j=2)), r=["ucar"], w=["ubuf"])
            A("act", f_acopy(csb[:, 0:TT], bC[:, 0:TT]), r=[nC], w=["csb"])
            A("act", f_acopy(tmpo[:, 512:512 + TT], bH[:, 0:TT]), r=[nH], w=["tmpo1"])
            A("act", f_acopy(tmpo[:, 0:TT], bB[:, 0:TT]), r=[nB], w=["tmpo0"])
            A("pool", f_tt(uv[:, :, 2:2 + L], v3(tmpo[:, 512:512 + TT]), v3(csb[:, 0:TT]), ALU.mult), r=["tmpo1", "csb"], w=["ubuf"])
            yv = v3(ycv[:, 0:TT])
            A("act", f_act(yv, uv[:, :, 0:L], AF.Copy, scale=wcT[:, 0, cc:cc + 1]), r=["ubuf", "wcT"], w=["ycv"])
            cv = v3(csb[:, 0:TT])
            for jj in (1, 2):
                A("act", f_act(cv, uv[:, :, jj:jj + L], AF.Copy, scale=wcT[:, jj, cc:cc + 1]), r=["ubuf", "wcT"], w=["csb"])
                A("pool", f_tt(yv, yv, cv, ALU.add), r=["ycv", "csb"], w=["ycv"])
            A("pool", f_tt(mixin[:, 4 + cc, 0:TT], tmpo[:, 0:TT], ycv[:, 0:TT], ALU.mult), r=["tmpo0", "ycv"], w=[f"mix{4 + cc}"])
            if kind == "p":
                if last_prompt:
                    A("pool", f_copy(uco[:, cc, 0:2], ubuf[:, 512:514]), r=["ubuf"], w=["uco"])
                else:
                    A("pool", f_copy(ucarry[:, cc, :], ubuf[:, 512:514]), r=["ubuf"], w=["ucarry"])
            else:
                A("pool", f_copy(uco[:, cc, :].rearrange("p (s j) -> p s j", j=2), uv[:, :, 8:10]), r=["ubuf"], w=["uco"])

        if kind == "p":
            warm(2.5, TR[:, :].bitcast(F32), "TR")
        for s in range(NS):
            A("pe", f_tr([(TR[:, h * 128:(h + 1) * 128], kendT[:, h, s * 128:(s + 1) * 128], ident) for h in range(4)]),
              r=[f"kendT{h}" for h in range(4)] + ["cB"], w=["TR"])
            A("act", f_acopy(kend_tok[:, s, :], TR[:, 0:512]), r=["TR"], w=[f"kend_tok{s}"])
        maskb = AP(cF, mcol, [[NCF, 128], [0, 4], [1, 128]])
        SLOTS = [(Sseq[:, 0, :], "Sseq0", []), (Sseq[:, 1, :], "Sseq1", []),
                 (xs2[:, 0, 0:512], "xs2a", ["xs2_0"]), (xs2[:, 0, 512:1024], "xs2b", ["xs2_0"]),
                 (xs2[:, 1, 0:512], "xs2c", ["xs2_1"]), (xs2[:, 1, 512:1024], "xs2d", ["xs2_1"])]

        km_done = set()

        def km_op(c_):
            if c_ in km_done:
                return
            km_done.add(c_)
            b_, i_c = c_ // NCB, c_ % NCB
            A("dve", f_ts(kend_m[:, c_ % 2, :], kend_tok[:, b_, :], cF[:, rcol + i_c:rcol + i_c + 1], None, ALU.mult),
              r=[f"kend_tok{b_}", "cF"], w=[f"kend_m{c_ % 2}"])

        TRf_ = TR[:, :].bitcast(F32)
        UBS = [(UPD, "UPD"), (SCb, "SC")]
        SBUFS = [(Sst[:], "Sst"), (Sseq[:, 0, :], "Sseq0")]

        def chain_block(blk, cg):
            t0 = blk * 128
            A("pe", f_mm([(TRf_[:, h * 128:(h + 1) * 128], kdec[:, h, t0:t0 + 128], qdec[:, h, t0:t0 + 128], True, True) for h in range(4)]),
              r=[f"kdec{h}" for h in range(4)] + [f"qdec{h}" for h in range(4)], w=["TR"])
            smv = sm[:, blk % 2, :]
            A("dve", f_tt(smv.rearrange("p (h t) -> p h t", t=128), TRf_[:, :].rearrange("p (h t) -> p h t", t=128), maskb, ALU.mult),
              r=["TR", "cF"], w=[f"sm{blk % 2}"])
            Ob = OB[blk % 2]
            oname = f"O{blk % 2}"
            Oth, othn = OB[(blk + 1) % 2], f"O{(blk + 1) % 2}"
            if kind == "p":
                warm(1.0, Oth, othn)
            A("pe", f_mm([(Ob[:, h * 128:(h + 1) * 128], vtok[:, blk, h * 128:(h + 1) * 128], smv[:, h * 128:(h + 1) * 128], h == 0, False)
                          for h in range(4)]), r=[f"vtok{blk}", f"sm{blk % 2}"], w=[oname])

            def upd_mm(c_, ub_, un_):
                A("pe", f_mm([(ub_[:, h * 128:(h + 1) * 128], kend_m[:, c_ % 2, h * 128:(h + 1) * 128], vtok[:, blk, h * 128:(h + 1) * 128], True, True)
                              for h in range(4)]), r=[f"kend_m{c_ % 2}", f"vtok{blk}"], w=[un_])

            def inter_mm(ci_, sb_ap, sbn_):
                A("pe", f_mm([(Ob[:, h * 128 + ci_ * C:h * 128 + (ci_ + 1) * C], sb_ap[:, h * 128:(h + 1) * 128],
                               qdec[:, h, t0 + ci_ * C:t0 + (ci_ + 1) * C], False, ci_ == NCB - 1) for h in range(4)]),
                  r=[sbn_] + [f"qdec{h}" for h in range(4)], w=[oname])

            if kind == "p":
                cs = [blk * NCB + ci for ci in range(NCB)]
                for c in cs:
                    km_op(c)
                for c in cs:
                    upd_mm(c, UBS[c % 2][0], UBS[c % 2][1])
                for ci, c in enumerate(cs):
                    if blk == NS - 1 and ci == 1:
                        warm(3.0, PB[0], "P0")
                    elif ci == 1:
                        warm(1.0, Oth, othn)
                    inter_mm(ci, Sb[:], "Sb")
                    if cg is not None:
                        for _ in range(2 if ci == 0 else 1):
                            next(cg, None)
                    (Sp, spn), (Sn, snn) = SBUFS[(c + 1) % 2], SBUFS[c % 2]
                    ub, uname = UBS[c % 2]
                    for h in range(4):
                        A("dve", f_stt(Sn[:, h * 128:(h + 1) * 128], Sp[:, h * 128:(h + 1) * 128], dec[:, h, c:c + 1],
                                       ub[:, h * 128:(h + 1) * 128], ALU.mult, ALU.add), r=[spn, "dec", uname], w=[snn])
                    A("act", f_acopy(Sb[:], Sn), r=[snn], w=["Sb"])
            else:
                for ci in range(NCB):
                    c = blk * NCB + ci
                    rot = c % 6
                    Scur, scname, extra = SLOTS[rot]
                    wl = [scname] + (extra if c < 6 else [])
                    A("sp", f_dma(Scur.rearrange("k (h v) -> k h v", v=128), srec_d[c].rearrange("h k v -> k h v")), w=wl, chan="auto")
                    Sbc, sbn = (Sb[:], "Sb") if c % 2 == 0 else (rl[:, 0, :], "rl0")
                    A("act", f_acopy(Sbc, Scur), r=[scname], w=[sbn])
                    inter_mm(ci, Sbc, sbn)
                    km_op(c)
                    ub, uname = UBS[c % 2]
                    upd_mm(c, ub, uname)
                    if cg is not None and ci < 2:
                        next(cg, None)
                    if c + 1 < NCH:
                        km_op(c + 1)
                    for h in range(4):
                        A("dve", f_stt(Scur[:, h * 128:(h + 1) * 128], Scur[:, h * 128:(h + 1) * 128], dec[:, h, c:c + 1],
                                       ub[:, h * 128:(h + 1) * 128], ALU.mult, ALU.add), r=[scname, "dec", uname], w=[scname])
                    A("pool", f_dma(recs_d[c].rearrange("h k v -> k h v"), Scur.rearrange("k (h v) -> k h v", v=128)), r=[scname], chan="auto")
            A("act", f_acopy(AP(arena, t0, [[8192, 128], [512, 4], [1, 128]]), Ob[:, :].rearrange("p (h t) -> p h t", t=128)),
              r=[oname], w=["A0", "A1", "A2", "A3"])

        def onorm(h):
            sqv = sq[:, h % 2, 0:TT]
            sqn = f"sq{h % 2}"
            A("act", f_act(sqv, oT(h)[:, 0:TT], AF.Square), r=[f"A{h}"], w=[sqn])
            bank, bname = [(OB[0], "O0"), (OB[1], "O1"), (UPD, "UPD"), (SCb, "SC")][h]
            A("pe", f_mm([(bank[:, 0:TT], onesb, sqv, True, True)]), r=[sqn, "cB"], w=[bname]); yield
            S4 = slab(4 + h)[:, 0:TT]; n4 = f"A{4 + h}"
            A("act", f_act(S4, bank[:, 0:TT], AF.Ln, bias=128.0 * EPS), r=[bname], w=[n4]); yield
            A("act", f_act(S4, S4, AF.Exp, scale=-0.5), r=[n4], w=[n4]); yield
            A("dve", f_tt(S4, oT(h)[:, 0:TT], S4, ALU.mult), r=[f"A{h}", n4], w=[n4]); yield
            A("dve", f_stt(mixin[:, h, 0:TT], S4, gon[:, h:h + 1], sg(h)[:, 0:TT], ALU.mult, ALU.mult), r=[n4, "gon", f"A{12 + h}"], w=[f"mix{h}"]); yield

        gens_on = None
        sched = [[0], [1], [2, 3], []] if kind == "p" else [[0], [1], [2], [3]]

        def chained(ccs):
            for cc_ in ccs:
                yield from conv_cc(cc_)

        for i_ in range(4):
            cg = chained(sched[i_])
            if i_ < NS:
                chain_block(i_, cg)
            if i_ == NS - 1:
                gens_on = [onorm(h) for h in range(4)]
                for g_ in gens_on:
                    next(g_)
            for _ in cg:
                pass
        ring_issue(); ring_issue(); ring_issue()
        if last_prompt or kind == "s":
            nr = 2 if kind == "p" else 32
            cb_, cbn = nextbank()
            A("pe", f_tr([(cb_[0:nr, cc * 128:(cc + 1) * 128], uco[:, cc, 0:nr], cF[:, IDF:IDF + 128]) for cc in range(4)]),
              r=["uco", "cF"], w=[cbn])
            A("act", f_acopy(ycv[0:nr, :], cb_[0:nr, :]), r=[cbn], w=["ycv"])
            dst = convp_d[:, :] if kind == "p" else convs_d.rearrange("s j c -> (s j) c")
            A("sp", f_dma(dst, ycv[0:nr, :]), r=["ycv"], chan="auto")
        if last_prompt:
            A("sp", f_dma(recp_d.rearrange("h k v -> k h v"), Sseq[:, 0, :].rearrange("k (h v) -> k h v", v=128)), r=["Sseq0"], chan="auto")

        if tidx == 0:
            setup_part2()
        bankset[0] = BANKS6
        bank_rr[0] = 0

        rr(gens_on)

        mixr = [f"mix{j}" for j in range(8)]
        so = [ring_take(("out", tidx, 0)), ring_take(("out", tidx, 1))]
        if kind == "p":
            warm(4.5, TR[:, :].bitcast(F32), "TR")
        for s in range(NS + 1):
            if s < NS:
                for dh in range(2):
                    bank, bname = nextbank()
                    A("pe", f_mm([(bank[:, :], mixin[:, j, s * 128:(s + 1) * 128], wr[:, so[dh], j, :], j == 0, j == 7) for j in range(8)]),
                      r=[f"wr{so[dh]}"] + mixr, w=[bname])
                    tv = tmpo[:, dh * 512:(dh + 1) * 512]
                    tn = f"tmpo{dh}"
                    A("dve", f_tt(tv, bank[:, :], G[:, dh * 512:(dh + 1) * 512], ALU.mult), r=[bname, gname], w=[tn])
                    A("pool", f_tt(xt[:, s, dh * 512:(dh + 1) * 512], xt[:, s, dh * 512:(dh + 1) * 512], tv, ALU.add), r=[f"xt{s}", tn], w=[f"xt{s}"])
                norm_stage1(xt[:, s, :], [f"xt{s}"], s, 4 + s)
            early_up = (kind == "p" and s == NS)
            if early_up:
                s_up0 = ring_take(("up", tidx, 0))
                up0_banks = [nextbank() for _ in range(4)]

                def up0(subs):
                    for cj in range(4):
                        bank, bname = up0_banks[cj]
                        for ss in subs:
                            A("pe", f_mm([(bank[:, ss * 128:(ss + 1) * 128], wr[:, s_up0, k, cj * 128:(cj + 1) * 128],
                                           hB[:, k, ss * 128:(ss + 1) * 128], k == 0, k == 7) for k in range(8)]),
                              r=[f"wr{s_up0}", f"hB{ss}"], w=[bname])
                up0([0, 1])
            if s >= 1:
                norm_stage2(kind, s - 1, hB, "hB", 24, 32, warm_us=(2.0 if (kind == "p" and s - 1 == 2) else 0.0))
            if early_up:
                up0([2, 3])
                for cj in range(4):
                    bank, bname = up0_banks[cj]
                    rv = rl[:, cj % 2, 0:TT]
                    rn = f"rl{cj % 2}"
                    A("act", f_act(rv, bank[:, 0:TT], AF.Relu), r=[bname], w=[rn])
                    A("dve", f_tt(uTf(cj)[:, 0:TT], rv, bank[:, 0:TT], ALU.mult), r=[rn, bname], w=[f"A{cj // 2}"])
        ring_issue(); ring_issue()
        if kind == "p":
            ring_issue()

        bankset[0] = BANKS6
        bank_rr[0] = 0
        if nxt is not None:
            A("dve", f_memset(st[:, 0:4], 0.0), w=[f"st{c_}" for c_ in range(4)])
        for ubk in range(8):
            if kind == "p" and ubk == 0:
                if nxt is not None:
                    n1_stage1(nxt[0], nxt[1], 0)
                continue
            s_ = ring_take(("up", tidx, ubk))
            for cj in range(4):
                fc = ubk * 4 + cj
                bank, bname = nextbank()
                A("pe", f_mm([(bank[:, 0:TT], wr[:, s_, k, cj * 128:(cj + 1) * 128], hB[:, k, 0:TT], k == 0, k == 7) for k in range(8)]),
                  r=[f"wr{s_}"] + HB, w=[bname])
                rv = rl[:, fc % 2, 0:TT]
                rn = f"rl{fc % 2}"
                A("act", f_act(rv, bank[:, 0:TT], AF.Relu), r=[bname], w=[rn])
                A("dve", f_tt(uTf(fc)[:, 0:TT], rv, bank[:, 0:TT], ALU.mult), r=[rn, bname], w=[f"A{fc // 2}"])
            ring_issue()
            if nxt is not None:
                nk, nti = nxt
                nns = 4 if nk == "p" else 1
                if ubk % 2 == 0 and ubk // 2 < nns:
                    n1_stage1(nk, nti, ubk // 2)
                if ubk % 2 == 1 and ubk // 2 < nns:
                    n1_stage2(nk, ubk // 2)

        TRf = TR[:, :].bitcast(F32)
        accs = [[(PB[0], "P0"), (PB[1], "P1"), (PB[2], "P2"), (SCb, "SC")], [(OB[0], "O0"), (OB[1], "O1"), (UPD, "UPD"), (TRf, "TR")]]
        for dh in range(2):
            for a in range(4):
                s_ = ring_take(("dn", tidx, dh, a))
                for s in range(NS):
                    bank, bname = accs[dh][s]
                    A("pe", f_mm([(bank[:, :], uTf(a * 8 + j)[:, s * 128:(s + 1) * 128], wr[:, s_, j, :], (a == 0 and j == 0), (a == 3 and j == 7))
                                  for j in range(8)]), r=[f"wr{s_}"] + [f"A{(a * 8 + j) // 2}" for j in range(0, 8, 2)], w=[bname])
                ring_issue()
            for s in range(NS):
                bank, bname = accs[dh][s]
                tv = tmpo[:, (s % 2) * 512:(s % 2 + 1) * 512]
                tn = f"tmpo{s % 2}"
                A("dve", f_tt(tv, bank[:, :], G[:, 1024 + dh * 512:1024 + (dh + 1) * 512], ALU.mult), r=[bname, gname], w=[tn])
                A("pool", f_tt(xt[:, s, dh * 512:(dh + 1) * 512], xt[:, s, dh * 512:(dh + 1) * 512], tv, ALU.add), r=[f"xt{s}", tn], w=[f"xt{s}"])
        bankset[0] = BANKS3
        bank_rr[0] = 0

        for s in range(NS):
            rstd_chain(xt[:, s, :], [f"xt{s}"], 8 + s)
            A("dve", f_stt(xt[:, s, :], xt[:, s, :], st[:, 8 + s:9 + s], gfin_b[:], ALU.mult, ALU.mult), r=[f"xt{s}", f"st{8 + s}", "gfin_b"], w=[f"xt{s}"])
            A("sp", f_dma(yd[s * 128:(s + 1) * 128, :], xt[:, s, :]), r=[f"xt{s}"], chan="auto")

    tiles = [("p", 0), ("p", 1), ("p", 2), ("p", 3), ("s", 0)]
    A("dve", f_memset(st[:, 0:4], 0.0), w=[f"st{c_}" for c_ in range(4)])
    for s in range(5):
        if s < 4:
            n1_stage1("p", 0, s)
        if s >= 1:
            n1_stage2("p", s - 1)
    for tidx, (k_, ti_) in enumerate(tiles):
        process_tile(k_, ti_, tidx, tiles[tidx + 1] if tidx + 1 < len(tiles) else None)
    assert ring["use"] == len(blocks), (ring["use"], len(blocks))
    P.final_wait("sp")
    P.emit()
    es.close()
    return nc


def _consts():
    cf = np.zeros((128, NCF), np.float32)
    p = np.arange(128)
    s_, t_ = p[:, None], p[None, :]
    cf[:, M64:M64 + 128] = ((s_ // 64 == t_ // 64) & (t_ >= s_)).astype(np.float32)
    cf[:, M8:M8 + 128] = ((s_ // 8 == t_ // 8) & (t_ >= s_)).astype(np.float32)
    sc = np.ones(512, np.float32); sc[::64] = 0
    cf[:, SC64:SC64 + 512] = sc[None, :]
    sc8 = np.ones(128, np.float32); sc8[::8] = 0
    cf[:, SC8:SC8 + 128] = sc8[None, :]
    for c in range(2):
        cf[:, R64 + c] = (p // 64 == c)
    for c in range(16):
        cf[:, R8 + c] = (p // 8 == c)
    cf[0, SELP:SELP + 128] = 1.0
    for q in range(128):
        cf[1 + q // 8, SELS + q] = 1.0
    cf[:, IDF:IDF + 128] = np.eye(128, dtype=np.float32)
    cb = np.zeros((128, 256), np.float32)
    cb[:, 0:128] = np.eye(128, dtype=np.float32)
    cb[:, 128:256] = 1.0
    return cf, cb


_CACHE = {}


def kernel(x_prompt, x_sample, state_rec, state_conv, c_prompt, c_sample, lower_bounds, w_ada, b_ada,
           w_in, w_conv, g_onorm, w_out, w_up, w_down, g_final):
    f = lambda a: np.ascontiguousarray(np.asarray(a, dtype=np.float32))
    x_prompt, x_sample, state_rec, state_conv = f(x_prompt), f(x_sample), f(state_rec), f(state_conv)
    c_prompt, c_sample = f(c_prompt), f(c_sample)
    if "nc" not in _CACHE:
        _CACHE["nc"] = build_program()
    nc = _CACHE["nc"]
    cf, cb = _consts()
    shared = dict(lbnd=f(lower_bounds), w_ada=f(w_ada)[0], b_ada=f(b_ada)[0], w_in=f(w_in)[0], w_conv=f(w_conv)[0],
                  g_onorm=f(g_onorm)[0], w_out=f(w_out)[0], w_up=f(w_up)[0], w_down=f(w_down)[0], g_final=f(g_final),
                  constF=cf, constB=cb)
    in_maps = []
    for c in range(NCORES):
        m = dict(shared)
        m["xp"] = x_prompt[c]
        m["xs"] = np.ascontiguousarray(x_sample[16 * c:16 * (c + 1)].reshape(128, 1024))
        m["srec"] = np.ascontiguousarray(state_rec[0, 16 * c:16 * (c + 1)])
        m["sconv"] = np.ascontiguousarray(state_conv[0, 16 * c:16 * (c + 1)])
        m["cc"] = np.ascontiguousarray(np.concatenate([c_prompt[c:c + 1], c_sample[16 * c:16 * (c + 1)]], axis=0))
        in_maps.append(m)
    res = run_bass_kernel_spmd(nc, in_maps, core_ids=list(range(NCORES)))
    R = res.results
    y_prompt = np.stack([R[c]["yp"] for c in range(NCORES)], axis=0)
    y_sample = np.concatenate([R[c]["ys"].reshape(16, 8, 1024) for c in range(NCORES)], axis=0)
    rec_p = np.stack([R[c]["recp"] for c in range(NCORES)], axis=0)[None]
    conv_p = np.stack([R[c]["convp"] for c in range(NCORES)], axis=0)[None]
    rec_s = np.concatenate([R[c]["recs"] for c in range(NCORES)], axis=0)[None]
    conv_s = np.concatenate([R[c]["convs"] for c in range(NCORES)], axis=0)[None]
    return (y_prompt.astype(np.float32), y_sample.astype(np.float32), rec_p.astype(np.float32),
            conv_p.astype(np.float32), rec_s.astype(np.float32), conv_s.astype(np.float32))
```

```python
import math
from contextlib import ExitStack

import numpy as np
import concourse.bass as bass
import concourse.mybir as mybir
from concourse.ap import AP
from concourse.bass_utils import run_bass_kernel_spmd

F32 = mybir.dt.float32
BF16 = mybir.dt.bfloat16
ALU = mybir.AluOpType
AF = mybir.ActivationFunctionType
EPS = 1e-6
NCORES = 8
RING = 6

M64, M8, SC64, SC8, R64, R8, SELP, SELS, IDF, NCF = 0, 128, 256, 768, 896, 898, 914, 1042, 1170, 1298

ENGS = ["pe", "act", "dve", "pool", "sp"]


class Buf:
    __slots__ = ("name", "w", "r")

    def __init__(self, name):
        self.name = name
        self.w = None
        self.r = []


class Chan:
    __slots__ = ("sem", "count", "name")

    def __init__(self, name):
        self.name = name
        self.sem = None
        self.count = 0


class Op:
    __slots__ = ("eng", "fn", "waits", "signal", "chan", "cnt")


class Prog:
    def __init__(self, nc):
        self.nc = nc
        self.ops = {e: [] for e in ENGS}
        self.chans = []
        self.bufs = {}
        self.auto = {}

    def b(self, name):
        x = self.bufs.get(name)
        if x is None:
            x = self.bufs[name] = Buf(name)
        return x

    def chan(self, name):
        c = Chan(name)
        self.chans.append(c)
        return c

    def op(self, eng, fn, r=(), w=(), chan=None):
        reads = [self.b(x) if isinstance(x, str) else x for x in r]
        writes = [self.b(x) if isinstance(x, str) else x for x in w]
        if chan == "auto":
            key = eng + ((":in:" + writes[0].name) if writes else (":out:" + reads[0].name))
            chan = self.auto.get(key)
            if chan is None:
                chan = self.auto[key] = self.chan("a%d" % len(self.auto))
        o = Op()
        o.eng = eng
        o.fn = fn
        o.signal = False
        o.chan = chan
        deps = {}
        cdeps = {}

        def add(d, war=False):
            if d is None:
                return
            if d.chan is not None:
                cdeps[d.chan] = d.chan.count
                return
            if d.eng == eng and eng == "pe":
                return
            cur = deps.get(d.eng)
            if cur is None or d.cnt > cur.cnt:
                deps[d.eng] = d

        for bb in reads:
            add(bb.w)
        for bb in writes:
            add(bb.w)
            for rr in bb.r:
                add(rr, war=True)
        o.waits = []
        for d in deps.values():
            d.signal = True
            o.waits.append(("op", d))
        for c, n in cdeps.items():
            o.waits.append(("chan", c, n * 16))
        if chan is not None:
            chan.count += 1
        for bb in writes:
            bb.w = o
            bb.r = []
        for bb in reads:
            if bb not in writes:
                bb.r.append(o)
        o.cnt = len(self.ops[eng]) + 1
        self.ops[eng].append(o)
        return o

    def final_wait(self, eng="sp"):
        o = Op()
        o.eng = eng
        o.fn = lambda e: e.nop()
        o.signal = False
        o.chan = None
        o.cnt = 0
        o.waits = [("chan", c, c.count * 16) for c in self.chans if c.count > 0]
        self.ops[eng].append(o)

    def emit(self):
        nc = self.nc
        with ExitStack() as es:
            esem = {}
            for e in ["pe", "act", "dve", "pool"]:
                esem[e] = es.enter_context(nc.semaphore("sem_" + e))
            for c in self.chans:
                c.sem = es.enter_context(nc.semaphore("ch_" + c.name))
            for e in ENGS:
                k = 0
                for o in self.ops[e]:
                    if o.signal and o.chan is None:
                        k += 1
                        o.cnt = k
                    else:
                        o.cnt = -1
            block = es.enter_context(nc.Block())

            def run(engname, h):
                waited = {}
                for o in self.ops[engname]:
                    need = {}
                    for wt in o.waits:
                        if wt[0] == "chan":
                            sem, val = wt[1].sem, wt[2]
                        else:
                            d = wt[1]
                            sem, val = esem[d.eng], d.cnt
                        k = sem.num
                        if k not in need or need[k][1] < val:
                            need[k] = (sem, val)
                    for k, (sem, val) in need.items():
                        if waited.get(k, 0) >= val:
                            continue
                        h.wait_ge(sem, val)
                        waited[k] = val
                    ins = o.fn(h)
                    if o.chan is not None:
                        ins.then_inc(o.chan.sem, 16)
                    elif o.signal:
                        ins.then_inc(esem[engname], 1)

            @block.tensor
            def _(t):
                run("pe", t)

            @block.scalar
            def _(t):
                run("act", t)

            @block.vector
            def _(t):
                run("dve", t)

            @block.gpsimd
            def _(t):
                run("pool", t)

            @block.sync
            def _(t):
                run("sp", t)


def f_mm(specs):
    specs = list(specs)

    def fn(e):
        ins = None
        for (o_, l_, r_, st, sp_) in specs:
            ins = e.matmul(o_, lhsT=l_, rhs=r_, start=st, stop=sp_, skip_group_check=True)
        return ins
    return fn


def f_tr(specs):
    specs = list(specs)

    def fn(e):
        ins = None
        for (o_, i_, id_) in specs:
            ins = e.transpose(out=o_, in_=i_, identity=id_)
        return ins
    return fn


def f_act(out, in_, func, scale=None, bias=None, accum=None):
    kw = {}
    if scale is not None:
        kw["scale"] = scale
    if bias is not None:
        kw["bias"] = bias
    if accum is not None:
        kw["accum_out"] = accum
    return lambda e: e.activation(out=out, in_=in_, func=func, **kw)


def f_tt(out, in0, in1, op):
    return lambda e: e.tensor_tensor(out=out, in0=in0, in1=in1, op=op)


def f_ts(out, in0, s1, s2, op0, op1=None):
    if op1 is None:
        return lambda e: e.tensor_scalar(out=out, in0=in0, scalar1=s1, scalar2=None, op0=op0)
    return lambda e: e.tensor_scalar(out=out, in0=in0, scalar1=s1, scalar2=s2, op0=op0, op1=op1)


def f_stt(out, in0, scalar, in1, op0, op1):
    return lambda e: e.scalar_tensor_tensor(out=out, in0=in0, scalar=scalar, in1=in1, op0=op0, op1=op1)


def f_scan(out, d0, d1):
    return lambda e: e.tensor_tensor_scan(out=out, data0=d0, data1=d1, initial=0.0, op0=ALU.mult, op1=ALU.add)


def f_copy(out, in_):
    return lambda e: e.tensor_copy(out=out, in_=in_)


def f_acopy(out, in_):
    return lambda e: e.activation(out=out, in_=in_, func=AF.Copy)


def f_dma(out, in_):
    return lambda e: e.dma_start(out=out, in_=in_)


def f_memset(ap, val):
    return lambda e: e.memset(ap, val)


def build_program():
    nc = bass.Bass("TRN2", target_bir_lowering=False)

    def din(name, shape):
        return nc.dram_tensor(name, shape, F32, kind="ExternalInput").ap()

    def dout(name, shape):
        return nc.dram_tensor(name, shape, F32, kind="ExternalOutput").ap()

    xp_d = din("xp", [2048, 1024]); xs_d = din("xs", [128, 1024])
    srec_d = din("srec", [16, 4, 128, 128]); sconv_d = din("sconv", [16, 2, 512])
    cc_d = din("cc", [17, 1024]); lbnd_d = din("lbnd", [2, 512])
    wada_d = din("w_ada", [1024, 6144]); bada_d = din("b_ada", [6144])
    win_d = din("w_in", [1024, 3584]); wconv_d = din("w_conv", [3, 512]); gon_d = din("g_onorm", [512])
    wout_d = din("w_out", [1024, 1024]); wup_d = din("w_up", [1024, 4096]); wdn_d = din("w_down", [4096, 1024])
    gfin_d = din("g_final", [1024]); cF_d = din("constF", [128, NCF]); cB_d = din("constB", [128, 256])
    yp_d = dout("yp", [2048, 1024]); ys_d = dout("ys", [128, 1024]); recp_d = dout("recp", [4, 128, 128])
    convp_d = dout("convp", [2, 512]); recs_d = dout("recs", [16, 4, 128, 128]); convs_d = dout("convs", [16, 2, 512])

    P = Prog(nc)
    A = P.op
    es = ExitStack()
    es.enter_context(nc.allow_non_contiguous_dma(reason="small one-time strided vector loads / state rows"))

    def sb(name, shape, dt=F32):
        return es.enter_context(nc.sbuf_tensor(name, shape, dt))

    def ps(name, shape, dt=F32):
        return es.enter_context(nc.psum_tensor(name, shape, dt))

    cF = sb("cF", [128, NCF]); cB = sb("cB", [128, 256], BF16)
    ident = cB[:, 0:128]; onesb = cB[:, 128:256]
    vecT = sb("vecT", [128, 72]); lbv = sb("lbv", [128, 4]); oml = sb("oml", [128, 4]); noml = sb("noml", [128, 4])
    badaT = vecT[:, 0:48]; lbT = vecT[:, 48:56].rearrange("p (r h) -> p r h", h=4)
    wcT = vecT[:, 56:68].rearrange("p (j c) -> p j c", c=4); gon = vecT[:, 68:72]
    ucar = sb("ucar", [128, 4, 32]); uco = sb("uco", [128, 4, 32])
    gfin_b = sb("gfin_b", [128, 1024]); scT = sb("scT", [128, 8, 17], BF16); modT = sb("modT", [128, 48, 17])
    Gp = sb("Gp", [128, 2048]); Gs = sb("Gs", [128, 2048])
    arena = sb("arena", [128, 8192])
    Sseq = sb("Sseq", [128, 2, 512])
    xt = sb("xt", [128, 4, 1024]); nb = sb("nb", [128, 2, 1024], BF16)
    xs2 = sb("xs2", [128, 2, 1024])
    hA = sb("hA", [128, 8, 512], BF16); hB = sb("hB", [128, 8, 512], BF16)
    qdec = sb("qdec", [128, 4, 512], BF16); kdec = sb("kdec", [128, 4, 512], BF16); kendT = sb("kendT", [128, 4, 512], BF16)
    kend_tok = sb("kend_tok", [128, 4, 512], BF16); kend_m = sb("kend_m", [128, 2, 512], BF16)
    vtok = sb("vtok", [128, 4, 512], BF16); sm = sb("sm", [128, 2, 512], BF16)
    njunk = sm[:, :, :].rearrange("p a b -> p (a b)")
    Sst = sb("Sst", [128, 512]); Sb = sb("Sb", [128, 512], BF16); sq = sb("sq", [128, 2, 512], BF16)
    mixin = sb("mixin", [128, 8, 512], BF16)
    csb = sb("csb", [128, 512]); ubuf = sb("ubuf", [128, 514]); ucarry = sb("ucarry", [128, 4, 2]); ycv = sb("ycv", [128, 512])
    rl = sb("rl", [128, 2, 512], BF16); tmpo = sb("tmpo", [128, 1024])
    dec = sb("dec", [128, 4, 16]); st = sb("st", [128, 24])
    wr = sb("wr", [128, RING, 8, 512], BF16)

    def slab(i):
        return arena[:, i * 512:(i + 1) * 512]

    def slabB(i):
        return P.b(f"A{i}")

    uTf_all = arena[:].bitcast(BF16)

    def uTf(fc):
        return uTf_all[:, fc * 512:(fc + 1) * 512]

    qsb = lambda h: slab(8 + h)
    sg = lambda h: slab(12 + h)
    oT = lambda h: slab(h)

    PB = [ps(f"P{i}", [128, 512]) for i in range(3)]
    TR = ps("TR", [128, 1024], BF16)
    OB = [ps(f"O{i}", [128, 512]) for i in range(2)]
    SCb = ps("SC", [128, 512]); UPD = ps("UPD", [128, 512])
    bank_rr = [0]

    BANKS3 = [(PB[0], "P0"), (PB[1], "P1"), (PB[2], "P2")]
    BANKS6 = BANKS3 + [(OB[0], "O0"), (OB[1], "O1"), (UPD, "UPD")]
    bankset = [BANKS3]

    def nextbank():
        bs = bankset[0]
        i = bank_rr[0] % len(bs)
        bank_rr[0] += 1
        return bs[i]

    HA = [f"hA{i}" for i in range(4)]
    HB = [f"hB{i}" for i in range(4)]

    def warm(us, bank, bname):
        n = max(1, int(round(us / 0.107)))
        A("pe", f_mm([(bank[:, 0:256], cB[:, 0:128], cB[:, 0:256], True, True)] * n), r=["cB"], w=[bname])

    def rr(gens):
        gens = list(gens)
        while gens:
            for g in list(gens):
                try:
                    next(g)
                except StopIteration:
                    gens.remove(g)


    def wv(w_ap, c0):
        return w_ap.rearrange("(k p) n -> p k n", p=128)[:, :, c0:c0 + 512]

    wdn_v = wdn_d.rearrange("(a k p) n -> a p k n", k=8, p=128)
    blocks = []
    for j in range(4):
        blocks.append((("ada", j), wv(wada_d, j * 512)))
    for ti in range(5):
        for j in range(7):
            blocks.append((("in", ti, j), wv(win_d, j * 512)))
        if ti == 0:
            for j in range(4, 12):
                blocks.append((("ada", j), wv(wada_d, j * 512)))
        for j in range(2):
            blocks.append((("out", ti, j), wv(wout_d, j * 512)))
        for j in range(8):
            blocks.append((("up", ti, j), wv(wup_d, j * 512)))
        for dh in range(2):
            for a in range(4):
                blocks.append((("dn", ti, dh, a), wdn_v[a][:, :, dh * 512:(dh + 1) * 512]))
    ring = {"issue": 0, "use": 0}
    wscr = nc.dram_tensor("wscr", [25, 128, 4096], BF16, kind="Internal").ap()
    ch_ws = [P.chan(f"ws{i}") for i in range(RING)]
    tile_blk = {}
    for n_, (key_, _) in enumerate(blocks):
        if key_[0] != "ada" and key_[1] in (3, 4):
            tile_blk.setdefault(key_[1], []).append(n_)

    def ring_issue():
        n = ring["issue"]
        if n >= len(blocks):
            return
        s = n % RING
        key = blocks[n][0]
        if key[0] != "ada" and key[1] == 4:
            b_ = tile_blk[4].index(n)
            A("pool", f_dma(wr[:, s].rearrange("p k n -> p (k n)"), wscr[b_]), r=[f"wscr{b_}"], w=[f"wr{s}"], chan="auto")
        else:
            A("pool", f_dma(wr[:, s], blocks[n][1]), w=[f"wr{s}"], chan="auto")
        ring["issue"] = n + 1

    def ring_take(key):
        n = ring["use"]
        assert blocks[n][0] == key, (blocks[n][0], key)
        ring["use"] = n + 1
        s = n % RING
        if key[0] != "ada" and key[1] == 3:
            b_ = tile_blk[3].index(n)
            A("sp", f_dma(wscr[b_], wr[:, s].rearrange("p k n -> p (k n)")), r=[f"wr{s}"], w=[f"wscr{b_}"], chan=ch_ws[s])
        return s

    A("sp", f_dma(cF[:], cF_d[:, :]), w=["cF"], chan="auto")
    A("pool", f_dma(cB[:], cB_d[:, :]), w=["cB"], chan="auto")
    for _ in range(RING):
        ring_issue()
    cin = arena[0:17, 0:1024]; ctmp = arena[0:17, 1024:2048]; gtok = arena[0:17, 2048:4096]
    A("sp", f_dma(cin, cc_d[:, :]), w=["A0", "A1"], chan="auto")
    A("dve", f_memset(csb[:, 0:256], 0.0), w=["csb"])
    A("dve", f_memset(ycv[:, :], 0.0), w=["ycv"])
    A("sp", f_dma(csb[0:48, 0:128], bada_d.rearrange("(c p) -> c p", p=128)), w=["csb"], chan="auto")
    A("sp", f_dma(csb[48:56, 0:128], lbnd_d.rearrange("r (h p) -> (r h) p", p=128)), w=["csb"], chan="auto")
    A("sp", f_dma(csb[56:64, 0:128], wconv_d.rearrange("j (c p) -> (j c) p", p=128)[0:8]), w=["csb"], chan="auto")
    A("sp", f_dma(csb[0:4, 128:256], wconv_d.rearrange("j (c p) -> (j c) p", p=128)[8:12]), w=["csb"], chan="auto")
    A("sp", f_dma(csb[4:8, 128:256], gon_d.rearrange("(h p) -> h p", p=128)), w=["csb"], chan="auto")
    A("sp", f_dma(ycv[0:32, :], sconv_d.rearrange("s j c -> (s j) c")), w=["ycv"], chan="auto")
    vb, vbn = nextbank()
    A("pe", f_tr([(vb[:, 0:128], csb[:, 0:128], cF[:, IDF:IDF + 128]),
                  (vb[:, 128:256], csb[:, 128:256], cF[:, IDF:IDF + 128])]), r=["csb", "cF"], w=[vbn])
    A("dve", f_copy(vecT[:, 0:64], vb[:, 0:64]), r=[vbn], w=["badaT", "lbT", "wcT", "gon"])
    A("dve", f_copy(vecT[:, 64:72], vb[:, 128:136]), r=[vbn], w=["badaT", "lbT", "wcT", "gon"])
    vb, vbn = nextbank()
    A("pe", f_tr([(vb[:, cc * 128:(cc + 1) * 128], ycv[:, cc * 128:(cc + 1) * 128], cF[:, IDF:IDF + 128]) for cc in range(4)]),
      r=["ycv", "cF"], w=[vbn])
    A("dve", f_copy(ucar[:], vb[:, :].rearrange("p (c t) -> p c t", t=128)[:, :, 0:32]), r=[vbn], w=["ucar"])
    A("sp", f_dma(gfin_b[:], gfin_d.partition_broadcast(128)), w=["gfin_b"], chan="auto")

    A("act", f_act(ctmp, cin, AF.Exp, scale=-1.0), r=["A0", "A1"], w=["A2", "A3"])
    A("act", f_act(ctmp, ctmp, AF.Ln, bias=1.0), r=["A2", "A3"], w=["A2", "A3"])
    A("act", f_act(ctmp, ctmp, AF.Exp, scale=-1.0), r=["A2", "A3"], w=["A2", "A3"])
    A("dve", f_tt(nb[0:17, 0, :], cin, ctmp, ALU.mult), r=["A0", "A1", "A2", "A3"], w=["nb0"])
    A("pe", f_tr([(TR[:, j * 32:j * 32 + 17], nb[0:17, 0, j * 128:(j + 1) * 128], cB[0:17, 0:17]) for j in range(8)]),
      r=["nb0", "cB"], w=["TR"])
    A("act", f_acopy(scT[:], TR[:, 0:256].rearrange("p (a b) -> p a b", b=32)[:, :, 0:17]), r=["TR"], w=["scT"])
    A("dve", f_tt(lbv[:], lbT[:, 1, :], lbT[:, 0, :], ALU.subtract), r=["lbT"], w=["lbv"])
    A("act", f_act(lbv[:], lbv[:], AF.Exp), r=["lbv"], w=["lbv"])
    A("act", f_act(lbv[:], lbv[:], AF.Ln, bias=1.0), r=["lbv"], w=["lbv"])
    A("act", f_act(lbv[:], lbv[:], AF.Exp, scale=-1.0), r=["lbv"], w=["lbv"])
    A("dve", f_ts(oml[:], lbv[:], -1.0, 1.0, ALU.mult, ALU.add), r=["lbv"], w=["oml"])
    A("dve", f_ts(noml[:], oml[:], -1.0, None, ALU.mult), r=["oml"], w=["noml"])
    A("dve", f_ts(gon[:], gon[:], math.sqrt(128.0), None, ALU.mult), r=["gon"], w=["gon"])
    A("dve", f_memset(ucarry[:], 0.0), w=["ucarry"])
    A("dve", f_memset(Sst[:], 0.0), w=["Sst"])
    A("dve", f_memset(Sseq[:, 0, :], 0.0), w=["Sseq0"])
    A("dve", f_memset(Sb[:], 0.0), w=["Sb"])

    FM_GROUPS = {0: (0, 0), 1: (0, 4), 2: (0, 8), 3: (0, 12), 6: (1, 0), 7: (1, 4), 8: (1, 8), 9: (1, 12)}
    fm_bank = {}
    gtok2 = xs2[0:17, :, :].rearrange("p a b -> p (a b)")

    def mod_blocks(bis):
        for bi in bis:
            s = ring_take(("ada", bi))
            if bi in FM_GROUPS:
                grp, c0 = FM_GROUPS[bi]
                if c0 == 0:
                    fm_bank[grp] = nextbank()
                bank, bname = fm_bank[grp]
                for cj in range(4):
                    col = (c0 + cj) * 17
                    A("pe", f_mm([(bank[:, col:col + 17], wr[:, s, k, cj * 128:(cj + 1) * 128], scT[:, k, :], k == 0, k == 7)
                                  for k in range(8)]), r=[f"wr{s}", "scT"], w=[bname])
                if c0 == 12:
                    j0 = 0 if grp == 0 else 24
                    A("dve", f_tt(modT[:, j0:j0 + 16, :], bank[:, 0:272].rearrange("p (a b) -> p a b", b=17),
                                  badaT[:, j0:j0 + 16].unsqueeze(2).to_broadcast([128, 16, 17]), ALU.add),
                      r=[bname, "badaT"], w=["modT"])
                    A("dve", f_ts(modT[:, j0 + 8:j0 + 16, :], modT[:, j0 + 8:j0 + 16, :], 1.0, None, ALU.add), r=["modT"], w=["modT"])
            else:
                half = {4: 0, 5: 1, 10: 2, 11: 3}[bi]
                bank, bname = nextbank()
                A("pe", f_mm([(bank[0:17, :], scT[:, k, :], wr[:, s, k, :], k == 0, k == 7) for k in range(8)]),
                  r=[f"wr{s}", "scT"], w=[bname])
                sl = [f"xs2_{half // 2}"]
                A("dve", f_tt(gtok2[:, half * 512:(half + 1) * 512], bank[0:17, :], gtok2[:, half * 512:(half + 1) * 512], ALU.add),
                  r=[bname] + sl, w=sl)
            ring_issue()

    def setup_part2():
        A("sp", f_dma(gtok2[:, 0:1024], bada_d[2048:3072].partition_broadcast(17)), w=["xs2_0"], chan="auto")
        A("sp", f_dma(gtok2[:, 1024:2048], bada_d[5120:6144].partition_broadcast(17)), w=["xs2_1"], chan="auto")
        mod_blocks(range(4, 12))
        for q in range(4):
            for (Gt, gname, sel0) in ((Gp, "Gp", SELP), (Gs, "Gs", SELS)):
                bank, bname = nextbank()
                A("pe", f_mm([(bank[:, :], cF[0:17, sel0:sel0 + 128], gtok2[:, q * 512:(q + 1) * 512], True, True)]),
                  r=["cF", f"xs2_{q // 2}"], w=[bname])
                A("act", f_acopy(Gt[:, q * 512:(q + 1) * 512], bank[:, :]), r=[bname], w=[gname])

    mod_blocks(range(0, 4))

    TRB = [(TR, "TR"), (SCb[:, :].bitcast(BF16), "SC")]

    def rstd_chain(src_ap, srcbufs, col):
        sn = f"st{col}"
        A("act", f_act(njunk, src_ap, AF.Square, accum=st[:, col:col + 1]), r=srcbufs + [sn], w=["sm0", "sm1", sn])
        A("act", f_act(st[:, col:col + 1], st[:, col:col + 1], AF.Ln, scale=1.0 / 1024, bias=EPS), r=[sn], w=[sn])
        A("act", f_act(st[:, col:col + 1], st[:, col:col + 1], AF.Exp, scale=-0.5), r=[sn], w=[sn])

    def norm_stage1(src_ap, srcbufs, s, col):
        rstd_chain(src_ap, srcbufs, col)
        A("act", f_act(nb[:, s % 2, :], src_ap, AF.Copy, scale=st[:, col:col + 1]), r=srcbufs + [f"st{col}"], w=[f"nb{s % 2}"])

    def norm_stage2(kind, s, hdst, hname, j_sh, j_sc, warm_us=0.0):
        trt, trn = TRB[s % 2]
        if warm_us > 0:
            warm(warm_us, TR[:, :].bitcast(F32) if trn == "TR" else SCb, trn)
        if trn == "TR":
            trap = lambda a, b: TR[:, a:b]
        else:
            trap = lambda a, b: trt[:, a:b]
        A("pe", f_tr([(trap(j * 128, (j + 1) * 128), nb[:, s % 2, j * 128:(j + 1) * 128], ident) for j in range(8)]),
          r=[f"nb{s % 2}", "cB"], w=[trn])
        if kind == "p":
            for j in range(8):
                A("dve", f_stt(hdst[:, j, s * 128:(s + 1) * 128], trap(j * 128, (j + 1) * 128), modT[:, j_sc + j, 0:1],
                               modT[:, j_sh + j, 0:1].to_broadcast([128, 128]), ALU.mult, ALU.add), r=[trn, "modT"], w=[f"{hname}{s}"])
        else:
            trv = trap(0, 1024).rearrange("p (j s t) -> p j s t", s=16, t=8)
            tmv = tmpo[:, :].rearrange("p (j s t) -> p j s t", s=16, t=8)
            hv = hdst[:, :, 0:128].rearrange("p j (s t) -> p j s t", t=8)
            scb = AP(modT, j_sc * 17 + 1, [[816, 128], [17, 8], [1, 16], [0, 8]])
            shb = AP(modT, j_sh * 17 + 1, [[816, 128], [17, 8], [1, 16], [0, 8]])
            A("dve", f_tt(tmv, trv, scb, ALU.mult), r=[trn, "modT", "tmpo0", "tmpo1"], w=["tmpo0", "tmpo1"])
            A("dve", f_tt(hv, tmv, shb, ALU.add), r=["tmpo0", "tmpo1", "modT"], w=[f"{hname}{s}"])

    def tile_x(kind, ti):
        return xp_d[ti * 512:(ti + 1) * 512, :] if kind == "p" else xs_d

    def n1_stage1(kind, ti, s):
        xd = tile_x(kind, ti)
        A("sp", f_dma(xs2[:, s % 2, :], xd[s * 128:(s + 1) * 128, :]), w=[f"xs2_{s % 2}"], chan="auto")
        norm_stage1(xs2[:, s % 2, :], [f"xs2_{s % 2}"], s, s)

    def n1_stage2(kind, s):
        norm_stage2(kind, s, hA, "hA", 0, 8)

    def process_tile(kind, ti, tidx, nxt):
        TT = 512 if kind == "p" else 128
        NS = TT // 128
        C = 64 if kind == "p" else 8
        NCH = TT // C
        NCB = 128 // C
        xd = xp_d[ti * 512:(ti + 1) * 512, :] if kind == "p" else xs_d
        yd = yp_d[ti * 512:(ti + 1) * 512, :] if kind == "p" else ys_d
        G = Gp if kind == "p" else Gs
        gname = "Gp" if kind == "p" else "Gs"
        mcol = M64 if kind == "p" else M8
        sccol = SC64 if kind == "p" else SC8
        rcol = R64 if kind == "p" else R8
        last_prompt = (kind == "p" and ti == 3)

        for s in range(NS):
            A("sp", f_dma(xt[:, s, :], xd[s * 128:(s + 1) * 128, :]), w=[f"xt{s}"], chan="auto")
        hT = hA
        bankset[0] = BANKS6
        bank_rr[0] = 0
        A("dve", f_memset(st[:, 4:12], 0.0), w=[f"st{c_}" for c_ in range(4, 12)])

        s_ = ring_take(("in", tidx, 0))
        for h in range(4):
            bank, bname = nextbank()
            A("pe", f_mm([(bank[:, 0:TT], wr[:, s_, k, h * 128:(h + 1) * 128], hT[:, k, 0:TT], k == 0, k == 7) for k in range(8)]),
              r=[f"wr{s_}"] + HA, w=[bname])
            A("act", f_acopy(qsb(h)[:, 0:TT], bank[:, 0:TT]), r=[bname], w=[f"A{8 + h}"])
        ring_issue()
        s_ = ring_take(("in", tidx, 1))
        if kind == "p":
            warm(1.5, TR[:, :].bitcast(F32), "TR")

        def fchain(h, bank, bname):
            b0 = 4 * (h % 2)
            S0, S1, S2, S3 = (slab(b0 + i)[:, 0:TT] for i in range(4))
            n0, n1, n2, n3 = (f"A{b0 + i}" for i in range(4))
            A("act", f_act(S0, bank[:, 0:TT], AF.Exp, scale=-1.0), r=[bname], w=[n0]); yield
            A("act", f_act(S1, S0, AF.Ln, bias=1.0), r=[n0], w=[n1]); yield
            A("act", f_act(S1, S1, AF.Exp, scale=-1.0), r=[n1], w=[n1]); yield
            A("act", f_act(S2, S1, AF.Ln, scale=oml[:, h:h + 1], bias=lbv[:, h:h + 1]), r=[n1, "oml", "lbv"], w=[n2])
            A("dve", f_ts(S3, S1, noml[:, h:h + 1], oml[:, h:h + 1], ALU.mult, ALU.add), r=[n1, "noml", "oml"], w=[n3]); yield
            A("dve", f_scan(S0, cF[:, sccol:sccol + TT], S2), r=["cF", n2], w=[n0]); yield
            A("act", f_act(S1, S0, AF.Exp), r=[n0], w=[n1])
            A("act", f_act(S2, S0, AF.Exp, scale=-1.0), r=[n0], w=[n2]); yield
            A("dve", f_tt(qdec[:, h, 0:TT], qsb(h)[:, 0:TT], S1, ALU.mult), r=[f"A{8 + h}", n1], w=[f"qdec{h}"])
            A("dve", f_tt(kdec[:, h, 0:TT], S3, S2, ALU.mult), r=[n3, n2], w=[f"kdec{h}"])
            cum3 = S0.rearrange("p (c t) -> p c t", t=C)
            A("act", f_act(dec[:, h, 0:NCH], cum3[:, :, C - 1], AF.Exp), r=[n0], w=["dec"]); yield
            A("dve", f_tt(S1.rearrange("p (c t) -> p c t", t=C), S2.rearrange("p (c t) -> p c t", t=C),
                          dec[:, h, 0:NCH].unsqueeze(2).to_broadcast([128, NCH, C]), ALU.mult), r=[n2, "dec", n1], w=[n1]); yield
            A("dve", f_tt(kendT[:, h, 0:TT], S3, S1, ALU.mult), r=[n3, n1], w=[f"kendT{h}"]); yield

        for hp in range(2):
            gens = []
            for h in (2 * hp, 2 * hp + 1):
                bank, bname = nextbank()
                A("pe", f_mm([(bank[:, 0:TT], wr[:, s_, k, h * 128:(h + 1) * 128], hT[:, k, 0:TT], k == 0, k == 7) for k in range(8)]),
                  r=[f"wr{s_}"] + HA, w=[bname])
                gens.append(fchain(h, bank, bname))
            rr(gens)
        ring_issue()
        if kind == "p":
            warm(2.0, TR[:, :].bitcast(F32), "TR")
        s_ = ring_take(("in", tidx, 2))
        for s in range(NS):
            bank, bname = nextbank()
            A("pe", f_mm([(bank[:, :], hT[:, k, s * 128:(s + 1) * 128], wr[:, s_, k, :], k == 0, k == 7) for k in range(8)]),
              r=[f"wr{s_}"] + HA, w=[bname])
            A("act", f_acopy(vtok[:, s, :], bank[:, :]), r=[bname], w=[f"vtok{s}"])
        ring_issue()
        s_ = ring_take(("in", tidx, 3))

        def gchain(h, bank, bname):
            si = [0, 1, 4, 5][h]
            S0 = slab(si)[:, 0:TT]
            n0 = f"A{si}"
            A("act", f_act(S0, bank[:, 0:TT], AF.Exp, scale=-1.0), r=[bname], w=[n0]); yield
            A("act", f_act(S0, S0, AF.Ln, bias=1.0), r=[n0], w=[n0]); yield
            A("act", f_act(S0, S0, AF.Exp, scale=-1.0), r=[n0], w=[n0]); yield
            A("dve", f_tt(sg(h)[:, 0:TT], bank[:, 0:TT], S0, ALU.mult), r=[bname, n0], w=[f"A{12 + h}"]); yield

        gens = []
        for h in range(4):
            bank, bname = nextbank()
            A("pe", f_mm([(bank[:, 0:TT], wr[:, s_, k, h * 128:(h + 1) * 128], hT[:, k, 0:TT], k == 0, k == 7) for k in range(8)]),
              r=[f"wr{s_}"] + HA, w=[bname])
            gens.append(gchain(h, bank, bname))
        rr(gens)
        ring_issue()
        bankset[0] = BANKS3
        bank_rr[0] = 0
        sB = ring_take(("in", tidx, 4)); sC = ring_take(("in", tidx, 5)); sH = ring_take(("in", tidx, 6))

        def conv_cc(cc):
            bB, nB = nextbank(); bC, nC = nextbank(); bH, nH = nextbank()
            for (bk, nm, sl) in ((bB, nB, sB), (bC, nC, sC), (bH, nH, sH)):
                A("pe", f_mm([(bk[:, 0:TT], wr[:, sl, k, cc * 128:(cc + 1) * 128], hT[:, k, 0:TT], k == 0, k == 7) for k in range(8)]),
                  r=[f"wr{sl}"] + HA, w=[nm])
                yield
            if kind == "p":
                uv = ubuf[:, :].unsqueeze(1)
                L = 512
                v3 = lambda t: t.unsqueeze(1)
                A("pool", f_copy(ubuf[:, 0:2], ucarry[:, cc, :]), r=["ucarry"], w=["ubuf"])
            else:
                uv = ubuf[:, 0:160].rearrange("p (s t) -> p s t", t=10)
                L = 8
                v3 = lambda t: t.rearrange("p (s t) -> p s t", t=8)
                A("pool", f_copy(uv[:, :, 0:2], ucar[:, cc, :].rearrange("p (s j) -> p s j", j=2)), r=["ucar"], w=["ubuf"])
            A("act", f_acopy(csb[:, 0:TT], bC[:, 0:TT]), r=[nC], w=["csb"])
            A("act", f_acopy(tmpo[:, 512:512 + TT], bH[:, 0:TT]), r=[nH], w=["tmpo1"])
            A("act", f_acopy(tmpo[:, 0:TT], bB[:, 0:TT]), r=[nB], w=["tmpo0"])
            A("pool", f_tt(uv[:, :, 2:2 + L], v3(tmpo[:, 512:512 + TT]), v3(csb[:, 0:TT]), ALU.mult), r=["tmpo1", "csb"], w=["ubuf"])
            yv = v3(ycv[:, 0:TT])
            A("act", f_act(yv, uv[:, :, 0:L], AF.Copy, scale=wcT[:, 0, cc:cc + 1]), r=["ubuf", "wcT"], w=["ycv"])
            cv = v3(csb[:, 0:TT])
            for jj in (1, 2):
                A("act", f_act(cv, uv[:, :, jj:jj + L], AF.Copy, scale=wcT[:, jj, cc:cc + 1]), r=["ubuf", "wcT"], w=["csb"])
                A("pool", f_tt(yv, yv, cv, ALU.add), r=["ycv", "csb"], w=["ycv"])
            A("pool", f_tt(mixin[:, 4 + cc, 0:TT], tmpo[:, 0:TT], ycv[:, 0:TT], ALU.mult), r=["tmpo0", "ycv"], w=[f"mix{4 + cc}"])
            if kind == "p":
                if last_prompt:
                    A("pool", f_copy(uco[:, cc, 0:2], ubuf[:, 512:514]), r=["ubuf"], w=["uco"])
                else:
                    A("pool", f_copy(ucarry[:, cc, :], ubuf[:, 512:514]), r=["ubuf"], w=["ucarry"])
            else:
                A("pool", f_copy(uco[:, cc, :].rearrange("p (s j) -> p s j", j=2), uv[:, :, 8:10]), r=["ubuf"], w=["uco"])

        if kind == "p":
            warm(2.5, TR[:, :].bitcast(F32), "TR")
        for s in range(NS):
            A("pe", f_tr([(TR[:, h * 128:(h + 1) * 128], kendT[:, h, s * 128:(s + 1) * 128], ident) for h in range(4)]),
              r=[f"kendT{h}" for h in range(4)] + ["cB"], w=["TR"])
            A("act", f_acopy(kend_tok[:, s, :], TR[:, 0:512]), r=["TR"], w=[f"kend_tok{s}"])
        maskb = AP(cF, mcol, [[NCF, 128], [0, 4], [1, 128]])
        SLOTS = [(Sseq[:, 0, :], "Sseq0", []), (Sseq[:, 1, :], "Sseq1", []),
                 (xs2[:, 0, 0:512], "xs2a", ["xs2_0"]), (xs2[:, 0, 512:1024], "xs2b", ["xs2_0"]),
                 (xs2[:, 1, 0:512], "xs2c", ["xs2_1"]), (xs2[:, 1, 512:1024], "xs2d", ["xs2_1"])]

        km_done = set()

        def km_op(c_):
            if c_ in km_done:
                return
            km_done.add(c_)
            b_, i_c = c_ // NCB, c_ % NCB
            A("dve", f_ts(kend_m[:, c_ % 2, :], kend_tok[:, b_, :], cF[:, rcol + i_c:rcol + i_c + 1], None, ALU.mult),
              r=[f"kend_tok{b_}", "cF"], w=[f"kend_m{c_ % 2}"])

        TRf_ = TR[:, :].bitcast(F32)
        UBS = [(UPD, "UPD"), (SCb, "SC")]
        SBUFS = [(Sst[:], "Sst"), (Sseq[:, 0, :], "Sseq0")]

        def chain_block(blk, cg):
            t0 = blk * 128
            A("pe", f_mm([(TRf_[:, h * 128:(h + 1) * 128], kdec[:, h, t0:t0 + 128], qdec[:, h, t0:t0 + 128], True, True) for h in range(4)]),
              r=[f"kdec{h}" for h in range(4)] + [f"qdec{h}" for h in range(4)], w=["TR"])
            smv = sm[:, blk % 2, :]
            A("dve", f_tt(smv.rearrange("p (h t) -> p h t", t=128), TRf_[:, :].rearrange("p (h t) -> p h t", t=128), maskb, ALU.mult),
              r=["TR", "cF"], w=[f"sm{blk % 2}"])
            Ob = OB[blk % 2]
            oname = f"O{blk % 2}"
            Oth, othn = OB[(blk + 1) % 2], f"O{(blk + 1) % 2}"
            if kind == "p":
                warm(1.0, Oth, othn)
            A("pe", f_mm([(Ob[:, h * 128:(h + 1) * 128], vtok[:, blk, h * 128:(h + 1) * 128], smv[:, h * 128:(h + 1) * 128], h == 0, False)
                          for h in range(4)]), r=[f"vtok{blk}", f"sm{blk % 2}"], w=[oname])

            def upd_mm(c_, ub_, un_):
                A("pe", f_mm([(ub_[:, h * 128:(h + 1) * 128], kend_m[:, c_ % 2, h * 128:(h + 1) * 128], vtok[:, blk, h * 128:(h + 1) * 128], True, True)
                              for h in range(4)]), r=[f"kend_m{c_ % 2}", f"vtok{blk}"], w=[un_])

            def inter_mm(ci_, sb_ap, sbn_):
                A("pe", f_mm([(Ob[:, h * 128 + ci_ * C:h * 128 + (ci_ + 1) * C], sb_ap[:, h * 128:(h + 1) * 128],
                               qdec[:, h, t0 + ci_ * C:t0 + (ci_ + 1) * C], False, ci_ == NCB - 1) for h in range(4)]),
                  r=[sbn_] + [f"qdec{h}" for h in range(4)], w=[oname])

            if kind == "p":
                cs = [blk * NCB + ci for ci in range(NCB)]
                for c in cs:
                    km_op(c)
                for c in cs:
                    upd_mm(c, UBS[c % 2][0], UBS[c % 2][1])
                for ci, c in enumerate(cs):
                    if blk == NS - 1 and ci == 1:
                        warm(3.0, PB[0], "P0")
                    elif ci == 1:
                        warm(1.0, Oth, othn)
                    inter_mm(ci, Sb[:], "Sb")
                    if cg is not None:
                        for _ in range(2 if ci == 0 else 1):
                            next(cg, None)
                    (Sp, spn), (Sn, snn) = SBUFS[(c + 1) % 2], SBUFS[c % 2]
                    ub, uname = UBS[c % 2]
                    for h in range(4):
                        A("dve", f_stt(Sn[:, h * 128:(h + 1) * 128], Sp[:, h * 128:(h + 1) * 128], dec[:, h, c:c + 1],
                                       ub[:, h * 128:(h + 1) * 128], ALU.mult, ALU.add), r=[spn, "dec", uname], w=[snn])
                    A("act", f_acopy(Sb[:], Sn), r=[snn], w=["Sb"])
            else:
                for ci in range(NCB):
                    c = blk * NCB + ci
                    rot = c % 6
                    Scur, scname, extra = SLOTS[rot]
                    wl = [scname] + (extra if c < 6 else [])
                    A("sp", f_dma(Scur.rearrange("k (h v) -> k h v", v=128), srec_d[c].rearrange("h k v -> k h v")), w=wl, chan="auto")
                    Sbc, sbn = (Sb[:], "Sb") if c % 2 == 0 else (rl[:, 0, :], "rl0")
                    A("act", f_acopy(Sbc, Scur), r=[scname], w=[sbn])
                    inter_mm(ci, Sbc, sbn)
                    km_op(c)
                    ub, uname = UBS[c % 2]
                    upd_mm(c, ub, uname)
                    if cg is not None and ci < 2:
                        next(cg, None)
                    if c + 1 < NCH:
                        km_op(c + 1)
                    for h in range(4):
                        A("dve", f_stt(Scur[:, h * 128:(h + 1) * 128], Scur[:, h * 128:(h + 1) * 128], dec[:, h, c:c + 1],
                                       ub[:, h * 128:(h + 1) * 128], ALU.mult, ALU.add), r=[scname, "dec", uname], w=[scname])
                    A("pool", f_dma(recs_d[c].rearrange("h k v -> k h v"), Scur.rearrange("k (h v) -> k h v", v=128)), r=[scname], chan="auto")
            A("act", f_acopy(AP(arena, t0, [[8192, 128], [512, 4], [1, 128]]), Ob[:, :].rearrange("p (h t) -> p h t", t=128)),
              r=[oname], w=["A0", "A1", "A2", "A3"])

        def onorm(h):
            sqv = sq[:, h % 2, 0:TT]
            sqn = f"sq{h % 2}"
            A("act", f_act(sqv, oT(h)[:, 0:TT], AF.Square), r=[f"A{h}"], w=[sqn])
            bank, bname = [(OB[0], "O0"), (OB[1], "O1"), (UPD, "UPD"), (SCb, "SC")][h]
            A("pe", f_mm([(bank[:, 0:TT], onesb, sqv, True, True)]), r=[sqn, "cB"], w=[bname]); yield
            S4 = slab(4 + h)[:, 0:TT]; n4 = f"A{4 + h}"
            A("act", f_act(S4, bank[:, 0:TT], AF.Ln, bias=128.0 * EPS), r=[bname], w=[n4]); yield
            A("act", f_act(S4, S4, AF.Exp, scale=-0.5), r=[n4], w=[n4]); yield
            A("dve", f_tt(S4, oT(h)[:, 0:TT], S4, ALU.mult), r=[f"A{h}", n4], w=[n4]); yield
            A("dve", f_stt(mixin[:, h, 0:TT], S4, gon[:, h:h + 1], sg(h)[:, 0:TT], ALU.mult, ALU.mult), r=[n4, "gon", f"A{12 + h}"], w=[f"mix{h}"]); yield

        gens_on = None
        sched = [[0], [1], [2, 3], []] if kind == "p" else [[0], [1], [2], [3]]

        def chained(ccs):
            for cc_ in ccs:
                yield from conv_cc(cc_)

        for i_ in range(4):
            cg = chained(sched[i_])
            if i_ < NS:
                chain_block(i_, cg)
            if i_ == NS - 1:
                gens_on = [onorm(h) for h in range(4)]
                for g_ in gens_on:
                    next(g_)
            for _ in cg:
                pass
        ring_issue(); ring_issue(); ring_issue()
        if last_prompt or kind == "s":
            nr = 2 if kind == "p" else 32
            cb_, cbn = nextbank()
            A("pe", f_tr([(cb_[0:nr, cc * 128:(cc + 1) * 128], uco[:, cc, 0:nr], cF[:, IDF:IDF + 128]) for cc in range(4)]),
              r=["uco", "cF"], w=[cbn])
            A("act", f_acopy(ycv[0:nr, :], cb_[0:nr, :]), r=[cbn], w=["ycv"])
            dst = convp_d[:, :] if kind == "p" else convs_d.rearrange("s j c -> (s j) c")
            A("sp", f_dma(dst, ycv[0:nr, :]), r=["ycv"], chan="auto")
        if last_prompt:
            A("sp", f_dma(recp_d.rearrange("h k v -> k h v"), Sseq[:, 0, :].rearrange("k (h v) -> k h v", v=128)), r=["Sseq0"], chan="auto")

        if tidx == 0:
            setup_part2()
        bankset[0] = BANKS6
        bank_rr[0] = 0

        rr(gens_on)

        mixr = [f"mix{j}" for j in range(8)]
        so = [ring_take(("out", tidx, 0)), ring_take(("out", tidx, 1))]
        if kind == "p":
            warm(4.5, TR[:, :].bitcast(F32), "TR")
        for s in range(NS + 1):
            if s < NS:
                for dh in range(2):
                    bank, bname = nextbank()
                    A("pe", f_mm([(bank[:, :], mixin[:, j, s * 128:(s + 1) * 128], wr[:, so[dh], j, :], j == 0, j == 7) for j in range(8)]),
                      r=[f"wr{so[dh]}"] + mixr, w=[bname])
                    tv = tmpo[:, dh * 512:(dh + 1) * 512]
                    tn = f"tmpo{dh}"
                    A("dve", f_tt(tv, bank[:, :], G[:, dh * 512:(dh + 1) * 512], ALU.mult), r=[bname, gname], w=[tn])
                    A("pool", f_tt(xt[:, s, dh * 512:(dh + 1) * 512], xt[:, s, dh * 512:(dh + 1) * 512], tv, ALU.add), r=[f"xt{s}", tn], w=[f"xt{s}"])
                norm_stage1(xt[:, s, :], [f"xt{s}"], s, 4 + s)
            early_up = (kind == "p" and s == NS)
            if early_up:
                s_up0 = ring_take(("up", tidx, 0))
                up0_banks = [nextbank() for _ in range(4)]

                def up0(subs):
                    for ss in subs:
                        for cj in range(4):
                            bank, bname = up0_banks[cj]
                            A("pe", f_mm([(bank[:, ss * 128:(ss + 1) * 128], wr[:, s_up0, k, cj * 128:(cj + 1) * 128],
                                           hB[:, k, ss * 128:(ss + 1) * 128], k == 0, k == 7) for k in range(8)]),
                              r=[f"wr{s_up0}", f"hB{ss}"], w=[bname])
                up0([0, 1])
            if s >= 1:
                norm_stage2(kind, s - 1, hB, "hB", 24, 32, warm_us=(2.0 if (kind == "p" and s - 1 == 2) else 0.0))
            if early_up:
                up0([2, 3])
                for cj in range(4):
                    bank, bname = up0_banks[cj]
                    rv = rl[:, cj % 2, 0:TT]
                    rn = f"rl{cj % 2}"
                    A("act", f_act(rv, bank[:, 0:TT], AF.Relu), r=[bname], w=[rn])
                    A("dve", f_tt(uTf(cj)[:, 0:TT], rv, bank[:, 0:TT], ALU.mult), r=[rn, bname], w=[f"A{cj // 2}"])
        ring_issue(); ring_issue()
        if kind == "p":
            ring_issue()

        bankset[0] = BANKS6
        bank_rr[0] = 0
        if nxt is not None:
            A("dve", f_memset(st[:, 0:4], 0.0), w=[f"st{c_}" for c_ in range(4)])
        for ubk in range(8):
            if kind == "p" and ubk == 0:
                if nxt is not None:
                    n1_stage1(nxt[0], nxt[1], 0)
                continue
            s_ = ring_take(("up", tidx, ubk))
            for cj in range(4):
                fc = ubk * 4 + cj
                bank, bname = nextbank()
                A("pe", f_mm([(bank[:, 0:TT], wr[:, s_, k, cj * 128:(cj + 1) * 128], hB[:, k, 0:TT], k == 0, k == 7) for k in range(8)]),
                  r=[f"wr{s_}"] + HB, w=[bname])
                rv = rl[:, fc % 2, 0:TT]
                rn = f"rl{fc % 2}"
                A("act", f_act(rv, bank[:, 0:TT], AF.Relu), r=[bname], w=[rn])
                A("dve", f_tt(uTf(fc)[:, 0:TT], rv, bank[:, 0:TT], ALU.mult), r=[rn, bname], w=[f"A{fc // 2}"])
            ring_issue()
            if nxt is not None:
                nk, nti = nxt
                nns = 4 if nk == "p" else 1
                if ubk % 2 == 0 and ubk // 2 < nns:
                    n1_stage1(nk, nti, ubk // 2)
                if ubk % 2 == 1 and ubk // 2 < nns:
                    n1_stage2(nk, ubk // 2)

        TRf = TR[:, :].bitcast(F32)
        accs = [[(PB[0], "P0"), (PB[1], "P1"), (PB[2], "P2"), (SCb, "SC")], [(OB[0], "O0"), (OB[1], "O1"), (UPD, "UPD"), (TRf, "TR")]]
        for dh in range(2):
            for a in range(4):
                s_ = ring_take(("dn", tidx, dh, a))
                for s in range(NS):
                    bank, bname = accs[dh][s]
                    A("pe", f_mm([(bank[:, :], uTf(a * 8 + j)[:, s * 128:(s + 1) * 128], wr[:, s_, j, :], (a == 0 and j == 0), (a == 3 and j == 7))
                                  for j in range(8)]), r=[f"wr{s_}"] + [f"A{(a * 8 + j) // 2}" for j in range(0, 8, 2)], w=[bname])
                ring_issue()
            for s in range(NS):
                bank, bname = accs[dh][s]
                tv = tmpo[:, (s % 2) * 512:(s % 2 + 1) * 512]
                tn = f"tmpo{s % 2}"
                A("dve", f_tt(tv, bank[:, :], G[:, 1024 + dh * 512:1024 + (dh + 1) * 512], ALU.mult), r=[bname, gname], w=[tn])
                A("pool", f_tt(xt[:, s, dh * 512:(dh + 1) * 512], xt[:, s, dh * 512:(dh + 1) * 512], tv, ALU.add), r=[f"xt{s}", tn], w=[f"xt{s}"])
        bankset[0] = BANKS3
        bank_rr[0] = 0

        for s in range(NS):
            rstd_chain(xt[:, s, :], [f"xt{s}"], 8 + s)
            A("dve", f_stt(xt[:, s, :], xt[:, s, :], st[:, 8 + s:9 + s], gfin_b[:], ALU.mult, ALU.mult), r=[f"xt{s}", f"st{8 + s}", "gfin_b"], w=[f"xt{s}"])
            A("sp", f_dma(yd[s * 128:(s + 1) * 128, :], xt[:, s, :]), r=[f"xt{s}"], chan="auto")

    tiles = [("p", 0), ("p", 1), ("p", 2), ("p", 3), ("s", 0)]
    A("dve", f_memset(st[:, 0:4], 0.0), w=[f"st{c_}" for c_ in range(4)])
    for s in range(5):
        if s < 4:
            n1_stage1("p", 0, s)
        if s >= 1:
            n1_stage2("p", s - 1)
    for tidx, (k_, ti_) in enumerate(tiles):
        process_tile(k_, ti_, tidx, tiles[tidx + 1] if tidx + 1 < len(tiles) else None)
    assert ring["use"] == len(blocks), (ring["use"], len(blocks))
    P.final_wait("sp")
    P.emit()
    es.close()
    return nc


def _consts():
    cf = np.zeros((128, NCF), np.float32)
    p = np.arange(128)
    s_, t_ = p[:, None], p[None, :]
    cf[:, M64:M64 + 128] = ((s_ // 64 == t_ // 64) & (t_ >= s_)).astype(np.float32)
    cf[:, M8:M8 + 128] = ((s_ // 8 == t_ // 8) & (t_ >= s_)).astype(np.float32)
    sc = np.ones(512, np.float32); sc[::64] = 0
    cf[:, SC64:SC64 + 512] = sc[None, :]
    sc8 = np.ones(128, np.float32); sc8[::8] = 0
    cf[:, SC8:SC8 + 128] = sc8[None, :]
    for c in range(2):
        cf[:, R64 + c] = (p // 64 == c)
    for c in range(16):
        cf[:, R8 + c] = (p // 8 == c)
    cf[0, SELP:SELP + 128] = 1.0
    for q in range(128):
        cf[1 + q // 8, SELS + q] = 1.0
    cf[:, IDF:IDF + 128] = np.eye(128, dtype=np.float32)
    cb = np.zeros((128, 256), np.float32)
    cb[:, 0:128] = np.eye(128, dtype=np.float32)
    cb[:, 128:256] = 1.0
    return cf, cb


_CACHE = {}


def kernel(x_prompt, x_sample, state_rec, state_conv, c_prompt, c_sample, lower_bounds, w_ada, b_ada,
           w_in, w_conv, g_onorm, w_out, w_up, w_down, g_final):
    f = lambda a: np.ascontiguousarray(np.asarray(a, dtype=np.float32))
    x_prompt, x_sample, state_rec, state_conv = f(x_prompt), f(x_sample), f(state_rec), f(state_conv)
    c_prompt, c_sample = f(c_prompt), f(c_sample)
    if "nc" not in _CACHE:
        _CACHE["nc"] = build_program()
    nc = _CACHE["nc"]
    cf, cb = _consts()
    shared = dict(lbnd=f(lower_bounds), w_ada=f(w_ada)[0], b_ada=f(b_ada)[0], w_in=f(w_in)[0], w_conv=f(w_conv)[0],
                  g_onorm=f(g_onorm)[0], w_out=f(w_out)[0], w_up=f(w_up)[0], w_down=f(w_down)[0], g_final=f(g_final),
                  constF=cf, constB=cb)
    in_maps = []
    for c in range(NCORES):
        m = dict(shared)
        m["xp"] = x_prompt[c]
        m["xs"] = np.ascontiguousarray(x_sample[16 * c:16 * (c + 1)].reshape(128, 1024))
        m["srec"] = np.ascontiguousarray(state_rec[0, 16 * c:16 * (c + 1)])
        m["sconv"] = np.ascontiguousarray(state_conv[0, 16 * c:16 * (c + 1)])
        m["cc"] = np.ascontiguousarray(np.concatenate([c_prompt[c:c + 1], c_sample[16 * c:16 * (c + 1)]], axis=0))
        in_maps.append(m)
    res = run_bass_kernel_spmd(nc, in_maps, core_ids=list(range(NCORES)))
    R = res.results
    y_prompt = np.stack([R[c]["yp"] for c in range(NCORES)], axis=0)
    y_sample = np.concatenate([R[c]["ys"].reshape(16, 8, 1024) for c in range(NCORES)], axis=0)
    rec_p = np.stack([R[c]["recp"] for c in range(NCORES)], axis=0)[None]
    conv_p = np.stack([R[c]["convp"] for c in range(NCORES)], axis=0)[None]
    rec_s = np.concatenate([R[c]["recs"] for c in range(NCORES)], axis=0)[None]
    conv_s = np.concatenate([R[c]["convs"] for c in range(NCORES)], axis=0)[None]
    return (y_prompt.astype(np.float32), y_sample.astype(np.float32), rec_p.astype(np.float32),
            conv_p.astype(np.float32), rec_s.astype(np.float32), conv_s.astype(np.float32))
```
